# Optimizing a Trainium2 kernel written in Bass

```python
import math
import jax, jax.numpy as jnp
from jax import lax
import numpy as np

D_MODEL = 1024
BATCH = 8
SEQ = 4096
DEPTH = 4

N_MIXERS = 3
D_PLE = 256
CONV_K = 4
CHUNK = 64
LN_EPS = 1e-5
DN_HEADS = 8
DN_DK = 128
DN_DV = 128
DN_QK = DN_HEADS * DN_DK
DN_V = DN_HEADS * DN_DV
DN_WIDTHS = (DN_QK, DN_QK, DN_V, DN_V, DN_HEADS, DN_HEADS)
DN_COLS = sum(DN_WIDTHS)
RW_HEAD = 64
RW_HEADS = D_MODEL // RW_HEAD
RW_W = RW_HEADS * RW_HEAD
RW_DECAY_LORA = 64
RW_A_LORA = 64
RW_GN_EPS = 64e-5
RW_WIDTHS = (RW_W, RW_DECAY_LORA, RW_W, RW_W, RW_A_LORA, RW_W)
RW_COLS = sum(RW_WIDTHS)
ML_HEADS = 8
ML_DQK = 64
ML_DV = 128
ML_QK = ML_HEADS * ML_DQK
ML_V = ML_HEADS * ML_DV
ML_WIDTHS = (ML_QK, ML_QK, ML_V, ML_V, ML_V, ML_HEADS, ML_HEADS)
ML_COLS = sum(ML_WIDTHS)
N_DN = (DEPTH + 2) // 3
N_RW = (DEPTH + 1) // 3
N_ML = DEPTH // 3
DEEPNORM_ALPHA = (2.0 * DEPTH) ** 0.25
DEEPNORM_BETA = (8.0 * DEPTH) ** -0.25

kernel_name = "hybrid_deltanet_rwkv7_mlstm_deepnorm"


def split_cols(h, widths):
    return jnp.split(h, [int(c) for c in np.cumsum(widths)[:-1]], axis=-1)


def layer_norm(x, g, b):
    xf = x.astype(jnp.float32)
    mu = jnp.mean(xf, -1, keepdims=True)
    var = jnp.mean(jnp.square(xf - mu), -1, keepdims=True)
    return ((xf - mu) * lax.rsqrt(var + LN_EPS) * g + b).astype(x.dtype)


def rms_norm(x, g, eps=1e-6):
    xf = x.astype(jnp.float32)
    return xf * lax.rsqrt(jnp.mean(xf * xf, -1, keepdims=True) + eps) * g


def l2_normalize(x, eps=1e-6):
    xf = x.astype(jnp.float32)
    return xf * lax.rsqrt(jnp.sum(xf * xf, -1, keepdims=True) + eps)


def group_norm(h, w, b, eps):
    mu = jnp.mean(h, -1, keepdims=True)
    var = jnp.mean(jnp.square(h - mu), -1, keepdims=True)
    out = ((h - mu) * lax.rsqrt(var + eps)).reshape(h.shape[0], h.shape[1], -1) * w
    return out if b is None else out + b


def causal_conv(x, w):
    K, S = w.shape[0], x.shape[1]
    xp = jnp.pad(x, ((0, 0), (K - 1, 0), (0, 0)))
    return sum(w[j] * xp[:, j:j + S] for j in range(K))


def token_shift(x):
    return jnp.pad(x, ((0, 0), (1, 0), (0, 0)))[:, :-1]


def gated_delta_rule_chunked(q, k, v, beta, g):
    B_, H, S, dk = q.shape
    dv = v.shape[-1]
    n, c = S // CHUNK, CHUNK
    q = q * dk ** -0.5
    to_chunks = lambda t: t.reshape(B_, H, n, c, *t.shape[3:])
    q, k, v, beta, g = map(to_chunks, (q, k, v, beta, g))
    G = jnp.cumsum(g, axis=-1)
    causal = jnp.tril(jnp.ones((c, c), dtype=bool))
    strict = jnp.tril(jnp.ones((c, c), dtype=bool), -1)
    diff = G[..., :, None] - G[..., None, :]
    decay_mat = jnp.where(causal, jnp.exp(jnp.where(causal, diff, 0.0)), 0.0)
    k_beta = k * beta[..., None]
    A = jnp.where(strict, jnp.einsum("bhnid,bhnjd->bhnij", k_beta, k) * decay_mat, 0.0)
    eye = jnp.eye(c, dtype=A.dtype)
    T = lax.linalg.triangular_solve(A + eye, jnp.broadcast_to(eye, A.shape),
                                    left_side=True, lower=True, unit_diagonal=True)
    u = jnp.einsum("bhnij,bhnjd->bhnid", T, v * beta[..., None])
    w = jnp.einsum("bhnij,bhnjd->bhnid", T, k_beta * jnp.exp(G)[..., None])
    qk = jnp.einsum("bhnid,bhnjd->bhnij", q, k) * decay_mat
    q_dec = q * jnp.exp(G)[..., None]
    k_dec = k * jnp.exp(G[..., -1:] - G)[..., None]
    g_last = jnp.exp(G[..., -1])

    def step(state, xs):
        qk_c, qd_c, kd_c, u_c, w_c, gl_c = xs
        v_new = u_c - jnp.einsum("bhid,bhde->bhie", w_c, state)
        o = jnp.einsum("bhid,bhde->bhie", qd_c, state) + jnp.einsum("bhij,bhje->bhie", qk_c, v_new)
        state = state * gl_c[..., None, None] + jnp.einsum("bhid,bhie->bhde", kd_c, v_new)
        return state, o

    xs = tuple(jnp.moveaxis(t, 2, 0) for t in (qk, q_dec, k_dec, u, w, g_last))
    state0 = jnp.zeros((B_, H, dk, dv), jnp.float32)
    _, o = lax.scan(step, state0, xs)
    return jnp.moveaxis(o, 0, 2).reshape(B_, H, S, dv)


def gated_deltanet_mixer(x, w_in, conv_w, a_log, dt_bias, norm_w, w_out):
    B_, S, _ = x.shape
    h = x @ w_in
    qkv, z, b_pre, a_pre = split_cols(h, (2 * DN_QK + DN_V, DN_V, DN_HEADS, DN_HEADS))
    qkv = jax.nn.silu(causal_conv(qkv, conv_w))
    q, k, v = split_cols(qkv, (DN_QK, DN_QK, DN_V))
    heads = lambda t, d: t.reshape(B_, S, DN_HEADS, d).transpose(0, 2, 1, 3)
    q = l2_normalize(heads(q, DN_DK))
    k = l2_normalize(heads(k, DN_DK))
    v = heads(v, DN_DV).astype(jnp.float32)
    beta = jax.nn.sigmoid(b_pre.astype(jnp.float32)).transpose(0, 2, 1)
    g = -(jnp.exp(a_log.astype(jnp.float32))
          * jax.nn.softplus(a_pre.astype(jnp.float32) + dt_bias)).transpose(0, 2, 1)
    o = gated_delta_rule_chunked(q, k, v, beta, g)
    o = rms_norm(o, norm_w).transpose(0, 2, 1, 3).reshape(B_, S, DN_V)
    y = (o * jax.nn.silu(z.astype(jnp.float32))).astype(x.dtype)
    return y @ w_out


def rwkv7_mixer(x, w_in, mu, w0, w_lora_up, a0, a_lora_up, k_k, k_a, r_k, gn_w, gn_b, w_out):
    B_, S, D = x.shape
    mu_cols = jnp.repeat(mu, np.array(RW_WIDTHS), axis=0, total_repeat_length=RW_COLS).T
    h = x @ w_in + (token_shift(x) - x) @ (mu_cols * w_in)
    r, w_lo, k, v, a_lo, z = split_cols(h, RW_WIDTHS)
    w_log = -jax.nn.softplus(-(w0 + jnp.tanh(w_lo) @ w_lora_up)) - 0.5
    a = jax.nn.sigmoid(a0 + a_lo @ a_lora_up)
    heads = lambda t: t.astype(jnp.float32).reshape(B_, S, RW_HEADS, RW_HEAD)
    r, k, v, a, w_log = map(heads, (r, k, v, a, w_log))
    kk = l2_normalize(k * k_k.reshape(RW_HEADS, RW_HEAD))
    k = k * (1.0 + (a - 1.0) * k_a.reshape(RW_HEADS, RW_HEAD))
    decay = jnp.exp(-jnp.exp(w_log))

    def step(state, xs):
        r_t, d_t, k_t, v_t, kk_t, a_t = xs
        sa = jnp.einsum("bhvk,bhk->bhv", state, -kk_t)
        state = (state * d_t[:, :, None, :] + sa[..., :, None] * (kk_t * a_t)[..., None, :]
                 + v_t[..., :, None] * k_t[..., None, :])
        return state, jnp.einsum("bhvk,bhk->bhv", state, r_t)

    xs = tuple(jnp.moveaxis(t, 1, 0) for t in (r, decay, k, v, kk, a))
    state0 = jnp.zeros((B_, RW_HEADS, RW_HEAD, RW_HEAD), jnp.float32)
    _, y = lax.scan(step, state0, xs)
    y = group_norm(jnp.moveaxis(y, 0, 1), gn_w, gn_b, RW_GN_EPS)
    bonus = jnp.sum(r * k * r_k, -1, keepdims=True) * v
    y = (y + bonus.reshape(B_, S, RW_W)) * jax.nn.silu(z.astype(jnp.float32))
    return y.astype(x.dtype) @ w_out


def mlstm_chunked(q, k, v, i_log, f_log):
    B_, H, S, dk = q.shape
    dv = v.shape[-1]
    n, c = S // CHUNK, CHUNK
    k = k * dk ** -0.5
    to_chunks = lambda t: t.reshape(B_, H, n, c, *t.shape[3:])
    q, k, v, i_log, f_log = map(to_chunks, (q, k, v, i_log, f_log))
    b = jnp.cumsum(f_log, axis=-1)
    causal = jnp.tril(jnp.ones((c, c), dtype=bool))
    d_log = jnp.where(causal, b[..., :, None] - b[..., None, :] + i_log[..., None, :], -jnp.inf)
    m_intra = jnp.max(d_log, axis=-1)
    qk = jnp.einsum("bhnid,bhnjd->bhnij", q, k)
    key_log = b[..., -1:] - b + i_log

    def step(carry, xs):
        C, nrm, m = carry
        qk_c, d_c, mi_c, b_c, kl_c, q_c, k_c, v_c = xs
        m_state = m[..., None] + b_c
        m_t = jnp.maximum(m_state, mi_c)
        inter = jnp.exp(m_state - m_t)
        w_qk = jnp.exp(d_c - m_t[..., None]) * qk_c
        num = (inter[..., None] * jnp.einsum("bhid,bhde->bhie", q_c, C)
               + jnp.einsum("bhij,bhje->bhie", w_qk, v_c))
        den = inter * jnp.einsum("bhid,bhd->bhi", q_c, nrm) + jnp.sum(w_qk, -1)
        h = num / jnp.maximum(jnp.abs(den), jnp.exp(-m_t))[..., None]
        m_new = m_t[..., -1]
        carry_scale = jnp.exp(m + b_c[..., -1] - m_new)
        k_w = k_c * jnp.exp(kl_c - m_new[..., None])[..., None]
        C = C * carry_scale[..., None, None] + jnp.einsum("bhid,bhie->bhde", k_w, v_c)
        nrm = nrm * carry_scale[..., None] + jnp.sum(k_w, axis=-2)
        return (C, nrm, m_new), h

    xs = tuple(jnp.moveaxis(t, 2, 0) for t in (qk, d_log, m_intra, b, key_log, q, k, v))
    carry0 = (jnp.zeros((B_, H, dk, dv), jnp.float32), jnp.zeros((B_, H, dk), jnp.float32),
              jnp.full((B_, H), -jnp.inf, jnp.float32))
    _, h = lax.scan(step, carry0, xs)
    return jnp.moveaxis(h, 0, 2).reshape(B_, H, S, dv)


def mlstm_mixer(x, w_in, conv_w, i_bias, f_bias, gn_w, w_out):
    B_, S, _ = x.shape
    h = x @ w_in
    qk, v, o, z, i_pre, f_pre = split_cols(h, (2 * ML_QK, ML_V, ML_V, ML_V, ML_HEADS, ML_HEADS))
    qk = jax.nn.silu(causal_conv(qk, conv_w))
    q, k = split_cols(qk, (ML_QK, ML_QK))
    heads = lambda t, d: t.astype(jnp.float32).reshape(B_, S, ML_HEADS, d).transpose(0, 2, 1, 3)
    q, k, v = heads(q, ML_DQK), heads(k, ML_DQK), heads(v, ML_DV)
    i_log = (i_pre.astype(jnp.float32) + i_bias).transpose(0, 2, 1)
    f_log = jax.nn.log_sigmoid(f_pre.astype(jnp.float32) + f_bias).transpose(0, 2, 1)
    h_tilde = mlstm_chunked(q, k, v, i_log, f_log).transpose(0, 2, 1, 3)
    og = jax.nn.sigmoid(o.astype(jnp.float32)).reshape(B_, S, ML_HEADS, ML_DV)
    hh = group_norm(og * h_tilde, gn_w, None, 1e-6)
    y = hh * jax.nn.silu(z.astype(jnp.float32))
    return y.astype(x.dtype) @ w_out


def setup_inputs(seed: int = 0) -> dict:
    key = jax.random.key(seed)
    ks = iter(jax.random.split(key, 40))
    nk = lambda: next(ks)
    nrm = lambda shape, s: jax.random.normal(nk(), shape, jnp.float32) * s
    uni = lambda shape, lo, hi: jax.random.uniform(nk(), shape, jnp.float32, lo, hi)
    D = D_MODEL

    def col_scale(widths, scales):
        return jnp.asarray(np.concatenate([np.full(wd, sc, np.float32) for wd, sc in zip(widths, scales)]))

    x = nrm((BATCH, SEQ, D), 1.0)
    p = nrm((DEPTH, BATCH, SEQ, D_PLE), 1.0)
    ln_g = 1.0 + nrm((DEPTH, D), 0.02)
    ln_b = nrm((DEPTH, D), 0.02)
    ple_w_proj = nrm((DEPTH, D_PLE, D), D_PLE ** -0.5)
    ple_norm_w = 1.0 + nrm((DEPTH, D), 0.02)
    ple_w_gate = nrm((DEPTH, D, D), D ** -0.5)

    s_in = D ** -0.5
    dn_w_in = nrm((N_DN, D, DN_COLS), 1.0) * col_scale(DN_WIDTHS, (s_in,) * 5 + (0.1 * s_in,))
    dn_conv_w = nrm((N_DN, CONV_K, 2 * DN_QK + DN_V), CONV_K ** -0.5)
    dn_a_log = jnp.log(uni((N_DN, DN_HEADS), 1.0, 16.0))
    dt = jnp.exp(uni((N_DN, DN_HEADS), math.log(1e-3), math.log(1e-1)))
    dn_dt_bias = dt + jnp.log(-jnp.expm1(-dt))
    dn_norm_w = 1.0 + nrm((N_DN, DN_DV), 0.02)
    dn_w_out = nrm((N_DN, DN_V, D), DN_V ** -0.5 * DEEPNORM_BETA)

    rw_w_in = nrm((N_RW, D, RW_COLS), s_in)
    rw_mu = uni((N_RW, len(RW_WIDTHS), D), 0.0, 1.0)
    rw_w0 = uni((N_RW, RW_W), -5.0, -1.0)
    rw_w_lora_up = nrm((N_RW, RW_DECAY_LORA, RW_W), 0.1 * RW_DECAY_LORA ** -0.5)
    rw_a0 = nrm((N_RW, RW_W), 0.1)
    rw_a_lora_up = nrm((N_RW, RW_A_LORA, RW_W), 0.1 * RW_A_LORA ** -0.5)
    rw_k_k = 0.85 + nrm((N_RW, RW_W), 0.05)
    rw_k_a = 1.0 + nrm((N_RW, RW_W), 0.05)
    rw_r_k = nrm((N_RW, RW_HEADS, RW_HEAD), 0.2)
    rw_gn_w = 1.0 + nrm((N_RW, RW_W), 0.02)
    rw_gn_b = nrm((N_RW, RW_W), 0.02)
    rw_w_out = nrm((N_RW, RW_W, D), RW_W ** -0.5 * DEEPNORM_BETA)

    ml_w_in = nrm((N_ML, D, ML_COLS), 1.0) * col_scale(ML_WIDTHS, (s_in,) * 5 + (0.1 * s_in,) * 2)
    ml_conv_w = nrm((N_ML, CONV_K, 2 * ML_QK), CONV_K ** -0.5)
    ml_i_bias = nrm((N_ML, ML_HEADS), 0.1)
    ml_f_bias = uni((N_ML, ML_HEADS), 3.0, 6.0)
    ml_gn_w = 1.0 + nrm((N_ML, ML_V), 0.02)
    ml_w_out = nrm((N_ML, ML_V, D), ML_V ** -0.5 * DEEPNORM_BETA)

    return {"x": x, "p": p, "ln_g": ln_g, "ln_b": ln_b, "ple_w_proj": ple_w_proj,
            "ple_norm_w": ple_norm_w, "ple_w_gate": ple_w_gate,
            "dn_w_in": dn_w_in, "dn_conv_w": dn_conv_w, "dn_a_log": dn_a_log, "dn_dt_bias": dn_dt_bias,
            "dn_norm_w": dn_norm_w, "dn_w_out": dn_w_out,
            "rw_w_in": rw_w_in, "rw_mu": rw_mu, "rw_w0": rw_w0, "rw_w_lora_up": rw_w_lora_up,
            "rw_a0": rw_a0, "rw_a_lora_up": rw_a_lora_up, "rw_k_k": rw_k_k, "rw_k_a": rw_k_a,
            "rw_r_k": rw_r_k, "rw_gn_w": rw_gn_w, "rw_gn_b": rw_gn_b, "rw_w_out": rw_w_out,
            "ml_w_in": ml_w_in, "ml_conv_w": ml_conv_w, "ml_i_bias": ml_i_bias, "ml_f_bias": ml_f_bias,
            "ml_gn_w": ml_gn_w, "ml_w_out": ml_w_out}


def reference(x, p, ln_g, ln_b, ple_w_proj, ple_norm_w, ple_w_gate,
              dn_w_in, dn_conv_w, dn_a_log, dn_dt_bias, dn_norm_w, dn_w_out,
              rw_w_in, rw_mu, rw_w0, rw_w_lora_up, rw_a0, rw_a_lora_up, rw_k_k, rw_k_a,
              rw_r_k, rw_gn_w, rw_gn_b, rw_w_out,
              ml_w_in, ml_conv_w, ml_i_bias, ml_f_bias, ml_gn_w, ml_w_out):
    for i in range(DEPTH):
        kind, j = i % N_MIXERS, i // N_MIXERS
        if kind == 0:
            y = gated_deltanet_mixer(x, dn_w_in[j], dn_conv_w[j], dn_a_log[j], dn_dt_bias[j],
                                     dn_norm_w[j], dn_w_out[j])
        elif kind == 1:
            y = rwkv7_mixer(x, rw_w_in[j], rw_mu[j], rw_w0[j], rw_w_lora_up[j], rw_a0[j], rw_a_lora_up[j],
                            rw_k_k[j], rw_k_a[j], rw_r_k[j], rw_gn_w[j], rw_gn_b[j], rw_w_out[j])
        else:
            y = mlstm_mixer(x, ml_w_in[j], ml_conv_w[j], ml_i_bias[j], ml_f_bias[j], ml_gn_w[j], ml_w_out[j])
        x = layer_norm(DEEPNORM_ALPHA * x + y, ln_g[i], ln_b[i])
        gate = jax.nn.sigmoid((x @ ple_w_gate[i]).astype(jnp.float32))
        x = (x.astype(jnp.float32) + gate * rms_norm(p[i] @ ple_w_proj[i], ple_norm_w[i])).astype(x.dtype)
    return x
```

```python
import math
from contextlib import ExitStack
import numpy as np
import concourse.bass as bass
import concourse.mybir as mybir
from concourse.bass_utils import run_bass_kernel_spmd

F32 = mybir.dt.float32
BF16 = mybir.dt.bfloat16
AF = mybir.ActivationFunctionType
ALU = mybir.AluOpType
AX = mybir.AxisListType

D = 1024
DEPTH = 4
DPLE = 256
C = 128
ALPHA = (2.0 * DEPTH) ** 0.25
LN_EPS = 1e-5
INV_F32 = {"dn": True, "rw": True}


class V:
    __slots__ = ("t", "ap")

    def __init__(self, t, ap):
        self.t = t
        self.ap = ap

    def __getitem__(self, idx):
        return V(self.t, self.ap[idx])

    def r(self, pat, **kw):
        return V(self.t, self.ap.rearrange(pat, **kw))

    def bc(self, shape):
        return V(self.t, self.ap.to_broadcast(shape))


class Tl:
    __slots__ = ("t", "w", "r", "name", "psum")

    def __init__(self, t, name, psum=False):
        self.t = t
        self.w = None
        self.r = {}
        self.name = name
        self.psum = psum

    def __getitem__(self, idx):
        return V(self, self.t[idx])

    @property
    def v(self):
        return V(self, self.t[:])


class K:
    def __init__(self, nc, st):
        self.nc = nc
        self.st = st
        self.eng = {"pe": nc.tensor, "act": nc.scalar, "dve": nc.vector, "pool": nc.gpsimd, "sp": nc.sync}
        self.sem = {}
        self.cnt = {}
        for e in ("pe", "act", "dve", "pool"):
            self.sem[e] = st.enter_context(nc.semaphore("c_" + e))
            self.cnt[e] = 0
        self.nq = {"sp": 16, "act": 2, "pool": 16}
        for q in ("sp", "act", "pool"):
            for i in range(self.nq[q]):
                nm = "q_%s%d" % (q, i)
                self.sem[nm] = st.enter_context(nc.semaphore(nm))
                self.cnt[nm] = 0
        self.qrr = {"sp": 0, "act": 0, "pool": 0}
        self.known = {e: {} for e in self.eng}
        self.uid = 0
        self.ninst = 0

    def sb(self, shape, dt, name=None):
        self.uid += 1
        name = "%s_%d" % (name or "sb", self.uid)
        return Tl(self.st.enter_context(self.nc.sbuf_tensor(name, list(shape), dt)), name)

    def ps(self, shape, dt, name=None):
        self.uid += 1
        name = "%s_%d" % (name or "ps", self.uid)
        return Tl(self.st.enter_context(self.nc.psum_tensor(name, list(shape), dt)), name, psum=True)

    def dram(self, name, shape, dt, kind):
        t = self.nc.dram_tensor(name, list(shape), dt, kind=kind)
        return Tl(t.ap(), name)

    def _deps(self, reads, writes):
        deps = {}

        def add(clk, val):
            if deps.get(clk, 0) < val:
                deps[clk] = val

        for t in reads:
            if t.w is not None:
                add(*t.w)
        for t in writes:
            if t.w is not None:
                add(*t.w)
            for clk, val in t.r.items():
                add(clk, val)
        return deps

    def _wait(self, e, deps, skip_own=False):
        kn = self.known[e]
        for clk, val in deps.items():
            if skip_own and clk == e:
                continue
            if kn.get(clk, 0) < val:
                self.eng[e].wait_ge(self.sem[clk], val)
                kn[clk] = val

    def op(self, e, fn, reads, writes):
        px = [t for t in reads if t.psum and t not in writes]
        if px:
            writes = list(writes) + px
        deps = self._deps(reads, writes)
        self._wait(e, deps, skip_own=(e == "pe"))
        inst = fn(self.eng[e])
        self.cnt[e] += 1
        inst.then_inc(self.sem[e], 1)
        self.ninst += 1
        c = self.cnt[e]
        for t in writes:
            t.w = (e, c)
            t.r = {}
        for t in reads:
            if t.r.get(e, 0) < c:
                t.r[e] = c

    def dma(self, q, out, in_):
        deps = self._deps([in_.t], [out.t])
        self._wait(q, deps)
        i = self.qrr[q]
        self.qrr[q] = (i + 1) % self.nq[q]
        nm = "q_%s%d" % (q, i)
        self._wait(q, {nm: self.cnt[nm]})
        inst = self.eng[q].dma_start(out=out.ap, in_=in_.ap)
        self.cnt[nm] += 16
        inst.then_inc(self.sem[nm], 16)
        self.ninst += 1
        c = self.cnt[nm]
        out.t.w = (nm, c)
        out.t.r = {}
        if in_.t.r.get(nm, 0) < c:
            in_.t.r[nm] = c

    def finish(self, tiles):
        deps = self._deps(tiles, [])
        self._wait("sp", deps)

    def mm(self, out, lhsT, rhs, start=True, stop=True):
        self.op("pe", lambda e: e.matmul(out.ap, lhsT.ap, rhs.ap, start=start, stop=stop),
                [lhsT.t, rhs.t], [out.t])

    def tr(self, out, in_, ident):
        self.op("pe", lambda e: e.transpose(out.ap, in_.ap, ident.ap), [in_.t, ident.t], [out.t])

    def act(self, out, in_, func, scale=1.0, bias=0.0, accum=None, eng="act"):
        reads = [in_.t]
        kw = {}
        if isinstance(scale, V):
            reads.append(scale.t)
            kw["scale"] = scale.ap
        else:
            kw["scale"] = scale
        if isinstance(bias, V):
            reads.append(bias.t)
            kw["bias"] = bias.ap
        else:
            kw["bias"] = bias
        writes = [out.t]
        if accum is not None:
            writes.append(accum.t)
            kw["accum_out"] = accum.ap
        self.op("act", lambda e: e.activation(out=out.ap, in_=in_.ap, func=func, **kw), reads, writes)

    def tt(self, eng, out, in0, in1, op):
        self.op(eng, lambda e: e.tensor_tensor(out=out.ap, in0=in0.ap, in1=in1.ap, op=op),
                [in0.t, in1.t], [out.t])

    def ts(self, eng, out, in0, s1, op0, s2=None, op1=None):
        reads = [in0.t]
        a1 = s1
        if isinstance(s1, V):
            reads.append(s1.t)
            a1 = s1.ap
        a2 = s2
        if isinstance(s2, V):
            reads.append(s2.t)
            a2 = s2.ap
        if op1 is None:
            self.op(eng, lambda e: e.tensor_scalar(out=out.ap, in0=in0.ap, scalar1=a1, scalar2=None, op0=op0),
                    reads, [out.t])
        else:
            self.op(eng, lambda e: e.tensor_scalar(out=out.ap, in0=in0.ap, scalar1=a1, scalar2=a2, op0=op0, op1=op1),
                    reads, [out.t])

    def stt(self, out, in0, scalar, in1, op0, op1):
        reads = [in0.t, in1.t]
        a = scalar
        if isinstance(scalar, V):
            reads.append(scalar.t)
            a = scalar.ap
        self.op("dve", lambda e: e.scalar_tensor_tensor(out=out.ap, in0=in0.ap, scalar=a, in1=in1.ap, op0=op0, op1=op1),
                reads, [out.t])

    def copy(self, eng, out, in_):
        if eng == "act":
            self.op("act", lambda e: e.copy(out=out.ap, in_=in_.ap), [in_.t], [out.t])
        else:
            self.op(eng, lambda e: e.tensor_copy(out=out.ap, in_=in_.ap), [in_.t], [out.t])

    def memset(self, eng, out, val):
        self.op(eng, lambda e: e.memset(out.ap, val), [], [out.t])

    def recip(self, out, in_):
        self.op("dve", lambda e: e.reciprocal(out=out.ap, in_=in_.ap), [in_.t], [out.t])

    def reduce(self, out, in_, op, axis=AX.X):
        self.op("dve", lambda e: e.tensor_reduce(out=out.ap, in_=in_.ap, axis=axis, op=op), [in_.t], [out.t])

    def scan(self, out, d0, d1, init, op0, op1):
        reads = [d0.t, d1.t]
        a = init
        if isinstance(init, V):
            reads.append(init.t)
            a = init.ap
        self.op("dve", lambda e: e.tensor_tensor_scan(out=out.ap, data0=d0.ap, data1=d1.ap, initial=a, op0=op0, op1=op1),
                reads, [out.t])

    def aselect(self, out, in_, pattern, cmp, fill, base, cm):
        self.op("pool", lambda e: e.affine_select(out=out.ap, in_=in_.ap, pattern=pattern, compare_op=cmp,
                                                   fill=fill, base=base, channel_multiplier=cm),
                [in_.t], [out.t])


class Consts:
    pass


class StopEmit(Exception):
    pass


import os as _os
_STAGE = int(_os.environ.get("KDBG_STAGE", "0"))


def stage(n):
    if _STAGE and n >= _STAGE:
        raise StopEmit()


def make_consts(k):
    c = Consts()
    c.ones_f = k.sb([128, 128], F32, "ones_f")
    k.memset("pool", c.ones_f.v, 1.0)
    c.ones_b = k.sb([128, 128], BF16, "ones_b")
    k.memset("pool", c.ones_b.v, 1.0)
    c.id_f = k.sb([128, 128], F32, "id_f")
    k.aselect(c.id_f.v, c.ones_f.v, [[-1, 128]], ALU.is_equal, 0.0, 0, 1)
    c.id_b = k.sb([128, 128], BF16, "id_b")
    k.copy("pool", c.id_b.v, c.id_f.v)
    c.iu_f = k.sb([128, 128], F32, "iu_f")
    k.aselect(c.iu_f.v, c.ones_f.v, [[1, 128]], ALU.is_ge, 0.0, 0, -1)
    c.iu_b = k.sb([128, 128], BF16, "iu_b")
    k.copy("pool", c.iu_b.v, c.iu_f.v)
    c.su_f = k.sb([128, 128], F32, "su_f")
    k.aselect(c.su_f.v, c.ones_f.v, [[1, 128]], ALU.is_gt, 0.0, 0, -1)
    c.su_b = k.sb([128, 128], BF16, "su_b")
    k.copy("pool", c.su_b.v, c.su_f.v)
    c.sl_f = k.sb([128, 128], F32, "sl_f")
    k.aselect(c.sl_f.v, c.ones_f.v, [[-1, 128]], ALU.is_gt, 0.0, 0, 1)
    return c


def bcast_load(k, dram_row, n, name, q="sp"):
    t = k.sb([128, n], F32, name)
    k.dma(q, t.v, V(dram_row.t, dram_row.ap.partition_broadcast(128)))
    return t


def col_load(k, dram_vec, n, name, base=0, tile=None, q="sp"):
    if tile is None:
        tile = k.sb([128, 1], F32, name)
    k.dma(q, tile[base:base + n, 0:1], V(dram_vec.t, dram_vec.ap.rearrange("(p o) -> p o", o=1)))
    return tile


class LayerW:
    pass


def load_common_weights(k, lw, li, din):
    ncol = lw.ncol
    lw.w_in = k.sb([128, 8, ncol], BF16, "w_in")
    src = din["w_in"]
    for kc in range(8):
        k.dma("pool", lw.w_in[:, kc, :], src[kc * 128:(kc + 1) * 128, :])
    lw.w_out = k.sb([128, 8, D], BF16, "w_out")
    for kc in range(8):
        k.dma("pool", lw.w_out[:, kc, :], din["w_out"][kc * 128:(kc + 1) * 128, :])
    lw.w_gate = k.sb([128, 8, D], BF16, "w_gate")
    for kc in range(8):
        k.dma("pool", lw.w_gate[:, kc, :], din["w_gate"][kc * 128:(kc + 1) * 128, :])
    lw.w_proj = k.sb([128, 2, D], BF16, "w_proj")
    for kc in range(2):
        k.dma("pool", lw.w_proj[:, kc, :], din["w_proj"][kc * 128:(kc + 1) * 128, :])
    lw.ln_g = bcast_load(k, din["ln_g"].v, D, "ln_g")
    lw.ln_b = bcast_load(k, din["ln_b"].v, D, "ln_b")
    lw.pn_w = bcast_load(k, din["pn_w"].v, D, "pn_w")


class Work:
    pass


def alloc_common(k, need_xT=True):
    w = Work()
    w.ps_mm = [k.ps([128, 512], F32, "psmm") for _ in range(2)]
    w.ps_tr = k.ps([128, 512], F32, "pstr")
    w.ps_trb = k.ps([128, 1024], BF16, "pstrb")
    w.x_tok = k.sb([128, D], F32, "x_tok")
    if need_xT:
        w.xT = k.sb([128, 8, 512], BF16, "xT")
    w.yT = k.sb([128, 8, 128], BF16, "yT")
    w.y_tok = k.sb([128, D], BF16, "y_tok")
    w.r_tok = k.sb([128, D], F32, "r_tok")
    w.xlnT = w.yT
    w.p_tok = k.sb([128, DPLE], F32, "p_tok")
    w.pT = k.sb([128, 2, 128], BF16, "pT")
    w.pp = k.sb([128, D], F32, "pp")
    w.st6 = k.sb([128, 2, 6], F32, "st6")
    w.mv = k.sb([128, 2], F32, "mv")
    w.sm = k.sb([128, 8], F32, "sm")
    return w


def load_x_tile(k, w, cst, x_in, t0, ntok):
    ns = ntok // 128
    for s in range(ns):
        k.dma("sp", w.x_tok.v, x_in[t0 + s * 128:t0 + (s + 1) * 128, :])
        for half in range(2):
            for j in range(4):
                kc = half * 4 + j
                k.tr(w.ps_tr[:, j * 128:(j + 1) * 128], w.x_tok[:, kc * 128:(kc + 1) * 128], cst.id_f.v)
            k.copy("act" if half == 0 else "dve", w.xT[:, half * 4:half * 4 + 4, s * 128:(s + 1) * 128],
                   w.ps_tr.v.r("p (a b) -> p a b", a=4))


def epilogue(k, w, cst, lw, x_in, y_tok, p_in, x_out, t0):
    k.dma("sp", w.x_tok.v, x_in[t0:t0 + 128, :])
    k.dma("sp", w.p_tok.v, p_in[t0:t0 + 128, :])
    k_tr_bf16_8(k, w, cst, y_tok, w.yT)
    for half in range(2):
        ps = w.ps_mm[half]
        for kc in range(8):
            k.mm(ps.v, w.yT[:, kc, :], lw.w_out[:, kc, half * 512:(half + 1) * 512], start=(kc == 0), stop=(kc == 7))
        k.stt(w.r_tok[:, half * 512:(half + 1) * 512], w.x_tok[:, half * 512:(half + 1) * 512], ALPHA, ps.v,
              ALU.mult, ALU.add)
    for half in range(2):
        k.op("dve", lambda e, half=half: e.bn_stats(out=w.st6.t[:, half, :], in_=w.r_tok.t[:, half * 512:(half + 1) * 512]),
             [w.r_tok], [w.st6])
    k.op("dve", lambda e: e.bn_aggr(out=w.mv.t[:, :], in_=w.st6.t[:].rearrange("p a b -> p (a b)")), [w.st6], [w.mv])
    k.act(w.sm[:, 0:1], w.mv[:, 1:2], AF.Sqrt, bias=LN_EPS_T(k), scale=1.0)
    k.recip(w.sm[:, 1:2], w.sm[:, 0:1])
    k.ts("dve", w.r_tok.v, w.r_tok.v, w.mv[:, 0:1], ALU.subtract, w.sm[:, 1:2], ALU.mult)
    k.tt("pool", w.r_tok.v, w.r_tok.v, lw.ln_g.v, ALU.mult)
    k.tt("pool", w.r_tok.v, w.r_tok.v, lw.ln_b.v, ALU.add)
    xln = w.r_tok
    for j in range(2):
        k.tr(w.ps_tr[:, j * 128:(j + 1) * 128], w.p_tok[:, j * 128:(j + 1) * 128], cst.id_f.v)
    k.copy("act", w.pT.v, w.ps_tr[:, 0:256].r("p (a b) -> p a b", a=2))
    for half in range(2):
        ps = w.ps_mm[half]
        for kc in range(2):
            k.mm(ps.v, w.pT[:, kc, :], lw.w_proj[:, kc, half * 512:(half + 1) * 512], start=(kc == 0), stop=(kc == 1))
        k.copy("act", w.pp[:, half * 512:(half + 1) * 512], ps.v)
    k.act(w.y_tok.v, w.pp.v, AF.Square, accum=w.sm[:, 2:3])
    k.act(w.sm[:, 3:4], w.sm[:, 2:3], AF.Sqrt, scale=1.0 / D, bias=EPS6_T(k))
    k.recip(w.sm[:, 4:5], w.sm[:, 3:4])
    k.stt(w.pp.v, w.pp.v, w.sm[:, 4:5], lw.pn_w.v, ALU.mult, ALU.mult)
    k.copy("act", w.y_tok.v, xln.v)
    k_tr_bf16_8(k, w, cst, w.y_tok.v, w.xlnT)
    for half in range(2):
        ps = w.ps_mm[half]
        hs = slice(half * 512, (half + 1) * 512)
        for kc in range(8):
            k.mm(ps.v, w.xlnT[:, kc, :], lw.w_gate[:, kc, hs], start=(kc == 0), stop=(kc == 7))
        k.act(ps.v, ps.v, AF.Sigmoid)
        k.tt("dve", w.pp[:, hs], ps.v, w.pp[:, hs], ALU.mult)
    k.tt("pool", w.pp.v, w.pp.v, xln.v, ALU.add)
    k.dma("sp", x_out[t0:t0 + 128, :], w.pp.v)


def k_tr_bf16_8(k, w, cst, src, dstT):
    for kc in range(8):
        k.tr(w.ps_trb[:, kc * 128:(kc + 1) * 128], src[:, kc * 128:(kc + 1) * 128], cst.id_b.v)
    k.copy("act", dstT.v, w.ps_trb.v.r("p (a b) -> p a b", a=8))


_eps_tiles = {}


def _eps_tile(k, val, nm):
    key = (id(k), nm)
    if key not in _eps_tiles:
        t = k.sb([128, 1], F32, nm)
        k.memset("pool", t.v, val)
        _eps_tiles[key] = t
    return _eps_tiles[key].v


def LN_EPS_T(k):
    return _eps_tile(k, LN_EPS, "eps_ln")


def EPS6_T(k):
    return _eps_tile(k, 1e-6, "eps6")


def tri_inverse(k, cst, Q0, P0, TT, ps, bufs, n=128, psd=None, f32=False):
    idt = cst.id_f if f32 else cst.id_b
    k.tt("pool", TT, idt[0:n, 0:n], Q0, ALU.subtract)
    k.mm(ps[0:n, 0:n], P0, Q0)
    k.mm(ps[0:n, n:2 * n], Q0, P0)
    k.copy("act", bufs[0][0:n, :, 0:n], ps[0:n, 0:2 * n].r("p (a b) -> p a b", a=2))
    Q, P = bufs[0][0:n, 0, 0:n], bufs[0][0:n, 1, 0:n]
    nlev = int(math.log2(n)) - 1
    if psd is None:
        psd = ps[0:n, 2 * n:3 * n]
    for s in range(1, nlev + 1):
        if s == nlev:
            k.mm(psd, P, TT)
            k.tt("dve", TT, psd, TT, ALU.add)
        else:
            k.mm(psd, P, TT)
            k.mm(ps[0:n, 0:n], P, Q)
            k.mm(ps[0:n, n:2 * n], Q, P)
            nb = bufs[s % 2]
            k.tt("dve", TT, psd, TT, ALU.add)
            k.copy("act", nb[0:n, :, 0:n], ps[0:n, 0:2 * n].r("p (a b) -> p a b", a=2))
            Q, P = nb[0:n, 0, 0:n], nb[0:n, 1, 0:n]


def conv_chunk(k, ps, xext, hist_c, cw_c, acc, first_tile):
    k.copy("act", xext[:, 3:515], ps)
    if first_tile:
        k.memset("pool", xext[:, 0:3], 0.0)
    else:
        k.copy("pool", xext[:, 0:3], hist_c)
    k.ts("dve", acc, xext[:, 3:515], cw_c[:, 3:4], ALU.mult)
    for j in (2, 1, 0):
        k.stt(acc, xext[:, j:j + 512], cw_c[:, j:j + 1], acc, ALU.mult, ALU.add)
    k.copy("pool", hist_c, xext[:, 512:515])


def emit_deltanet(k, cst, w, lw, din, x_in, p_in, x_out, S):
    H = 8
    NT = S // 512
    cw = k.sb([128, 24, 4], F32, "cw")
    k.dma("sp", cw.v, din["cw"].v)
    wg = k.sb([128, 8, 96], BF16, "wg")
    for kc in range(8):
        k.dma("pool", wg[:, kc, :], din["wg"][kc * 128:(kc + 1) * 128, :])
    gp = k.sb([128, 2], F32, "gp")
    k.dma("sp", gp[0:96, :], din["gp"].v)
    nA = k.sb([128, 1], F32, "nA")
    k.act(nA[0:96, :], gp[0:96, 1:2], AF.Exp)
    k.ts("pool", nA[0:96, :], nA[0:96, :], -1.0, ALU.mult)
    nw = bcast_load(k, din["nw"].v, 128, "nw")

    qT = k.sb([128, H, 512], BF16, "qT")
    kT = k.sb([128, H, 512], BF16, "kT")
    vT = k.sb([128, H, 512], BF16, "vT")
    xext = [k.sb([128, 515], F32, "xext")] * 2
    acc = [k.sb([128, 512], F32, "acc") for _ in range(2)]
    sq = [k.sb([128, 512], BF16, "sq")] * 2
    hist = k.sb([128, 24, 3], F32, "hist")
    zs = k.sb([128, D], BF16, "zs")
    GATE = k.sb([128, 512], F32, "GATE")
    k.memset("pool", GATE.v, 0.0)
    gtok = k.sb([128, 96], F32, "gtok")
    egp = k.sb([128, 8], F32, "egp")
    egt = k.sb([128, 8], F32, "egt")
    kd = k.sb([128, 8], F32, "kd")
    egl = k.sb([128, 8], F32, "egl")
    vtok = k.sb([128, H, 128], BF16, "vtok")
    kdec = k.sb([128, H, 128], BF16, "kdec")
    Sst = k.sb([128, H, 128], F32, "Sst")
    Sbf = k.sb([128, H, 128], BF16, "Sbf")
    k.memset("pool", Sst.v, 0.0)
    k.memset("pool", Sbf.v, 0.0)
    o_tok = k.sb([128, H, 128], F32, "o_tok")
    oss = k.sb([128, 8], F32, "oss")
    NSET = 2
    gtri = [k.sb([128, 128], F32, "gtri") for _ in range(NSET)]
    o1s = [k.sb([128, 128], F32, "o1s") for _ in range(NSET)]
    Draw = [k.sb([128, 128], BF16, "Draw") for _ in range(NSET)]
    Dincl = [k.sb([128, 128], BF16, "Dincl") for _ in range(NSET)]
    Dstr = [k.sb([128, 128], BF16, "Dstr") for _ in range(NSET)]
    f32i = INV_F32["dn"]
    IDT = F32 if f32i else BF16
    Q0 = [k.sb([128, 128], IDT, "Q0") for _ in range(NSET)]
    P0 = [k.sb([128, 128], IDT, "P0") for _ in range(NSET)]
    MT = [k.sb([128, 128], BF16, "MT") for _ in range(NSET)]
    TT = [k.sb([128, 128], IDT, "TT") for _ in range(NSET)]
    Rt = [k.sb([128, 128], IDT, "Rt") for _ in range(NSET)]
    vnew = [k.sb([128, 128], BF16, "vnew") for _ in range(NSET)]
    ibufs = [[k.sb([128, 2, 128], IDT, "ib") for _ in range(2)] for _ in range(NSET)]
    ps_inv = [k.ps([128, 512], F32, "psinv") for _ in range(NSET)]
    ps_misc = [k.ps([128, 512], F32, "psmisc") for _ in range(NSET)]

    stage(1)
    for ti in range(NT):
        t0 = ti * 512
        load_x_tile(k, w, cst, x_in, t0, 512)
        stage(2)
        ps = w.ps_mm[0]
        for kc in range(8):
            k.mm(ps[0:96, :], wg[:, kc, :], w.xT[:, kc, :], start=(kc == 0), stop=(kc == 7))
        k.act(GATE[0:8, :], ps[0:8, :], AF.Sigmoid)
        tmpg = acc[0]
        for base in (32, 64):
            sl = slice(base, base + 8)
            k.act(tmpg[sl, :], ps[sl, :], AF.Exp, bias=gp[sl, 0:1])
            k.act(tmpg[sl, :], tmpg[sl, :], AF.Ln, bias=1.0)
            k.ts("dve", GATE[sl, :], tmpg[sl, :], nA[sl, 0:1], ALU.mult)
        for c in range(4):
            cs = slice(c * 128, (c + 1) * 128)
            k.scan(GATE[64:72, cs], cst.ones_f[64:72, :], GATE[64:72, cs], 0.0, ALU.mult, ALU.add)
        stage(3)
        for cc in range(24):
            b = cc % 2
            ps = w.ps_mm[cc % 2]
            for kc in range(8):
                k.mm(ps.v, lw.w_in[:, kc, cc * 128:(cc + 1) * 128], w.xT[:, kc, :], start=(kc == 0), stop=(kc == 7))
            conv_chunk(k, ps.v, xext[b].v, hist[:, cc, :], cw[:, cc, :], acc[b].v, ti == 0)
            if cc >= 16:
                k.act(vT[:, cc - 16, :], acc[b].v, AF.Silu)
                continue
            k.act(acc[b].v, acc[b].v, AF.Silu)
            k.tt("pool", sq[b].v, acc[b].v, acc[b].v, ALU.mult)
            ps2 = w.ps_mm[(cc + 1) % 2]
            k.mm(ps2.v, cst.ones_b.v, sq[b].v)
            rs = xext[b][:, 0:512]
            if cc < 8:
                k.act(rs, ps2.v, AF.Sqrt, scale=128.0, bias=EPSQ_T(k))
                k.recip(rs, rs)
                k.tt("pool", qT[:, cc, :], acc[b].v, rs, ALU.mult)
            else:
                k.act(rs, ps2.v, AF.Sqrt, scale=1.0, bias=EPS6_T(k))
                k.recip(rs, rs)
                k.tt("pool", kT[:, cc - 8, :], acc[b].v, rs, ALU.mult)
        stage(4)
        for c in range(4):
            cs = slice(c * 128, (c + 1) * 128)
            for half in range(2):
                ps = w.ps_mm[half]
                for kc in range(8):
                    k.mm(ps.v, w.xT[:, kc, cs], lw.w_in[:, kc, 3072 + half * 512:3072 + (half + 1) * 512],
                         start=(kc == 0), stop=(kc == 7))
                k.act(zs[:, half * 512:(half + 1) * 512], ps.v, AF.Silu)
            k.tr(w.ps_tr[:, 0:96], GATE[0:96, cs], cst.id_f[0:96, 0:96])
            k.copy("act", gtok.v, w.ps_tr[:, 0:96])
            k.mm(w.ps_tr[:, 128:136], cst.ones_f.v, gtok[:, 32:40])
            k.act(egp.v, gtok[:, 64:72], AF.Exp)
            k.ts("pool", egt.v, egp.v, -1.0, ALU.mult)
            k.tt("dve", kd.v, w.ps_tr[:, 128:136], gtok[:, 64:72], ALU.subtract)
            k.act(kd.v, kd.v, AF.Exp)
            k.act(egl.v, w.ps_tr[:, 128:136], AF.Exp)
            for h in range(H):
                k.tr(w.ps_trb[:, h * 128:(h + 1) * 128], vT[:, h, cs], cst.id_b.v)
            k.copy("act", vtok.v, w.ps_trb.v.r("p (a b) -> p a b", a=H))
            for h in range(H):
                k.tr(w.ps_trb[:, h * 128:(h + 1) * 128], kT[:, h, cs], cst.id_b.v)
            k.tt("dve", kdec.v, w.ps_trb.v.r("p (a b) -> p a b", a=H), kd.v.r("p (h o) -> p h o", o=1).bc([128, H, 128]),
                 ALU.mult)
            stage(5)
            for h in range(H):
                b = h % NSET
                pm = ps_misc[b]
                k.ts("pool", gtri[b].v, cst.iu_f.v, gtok[:, 32 + h:33 + h], ALU.mult)
                k.mm(pm[:, 0:128], cst.sl_f.v, gtri[b].v)
                k.act(Draw[b].v, pm[:, 0:128], AF.Exp)
                k.tt("pool", Dincl[b].v, Draw[b].v, cst.iu_b.v, ALU.mult)
                k.tt("pool", Dstr[b].v, Draw[b].v, cst.su_b.v, ALU.mult)
                k.mm(pm[:, 128:256], kT[:, h, cs], kT[:, h, cs])
                k.stt(Q0[b].v, pm[:, 128:256], gtok[:, h:h + 1], Dstr[b].v, ALU.mult, ALU.mult)
                k.mm(pm[:, 256:384], kT[:, h, cs], qT[:, h, cs])
                k.tt("dve", MT[b].v, pm[:, 256:384], Dincl[b].v, ALU.mult)
                pi = ps_inv[b]
                if f32i:
                    k.tr(w.ps_tr[:, 0:128], Q0[b].v, cst.id_f.v)
                    k.copy("act", P0[b].v, w.ps_tr[:, 0:128])
                else:
                    k.tr(w.ps_trb[:, 0:128], Q0[b].v, cst.id_b.v)
                    k.copy("act", P0[b].v, w.ps_trb[:, 0:128])
                stage(6)
                tri_inverse(k, cst, Q0[b].v, P0[b].v, TT[b].v, pi, ibufs[b], psd=pm[:, 384:512], f32=f32i)
                stage(7)
                pc = w.ps_mm[b]
                k.mm(pc[:, 0:128], kT[:, h, cs], Sbf[:, h, :])
                k.stt(Rt[b].v, pc[:, 0:128], egt[:, h:h + 1], vtok[:, h, :], ALU.mult, ALU.add)
                k.mm(pc[:, 128:256], TT[b].v, Rt[b].v)
                k.act(vnew[b].v, pc[:, 128:256], AF.Copy, scale=gtok[:, h:h + 1])
                k.mm(pc[:, 256:384], qT[:, h, cs], Sbf[:, h, :])
                k.mm(pc[:, 384:512], MT[b].v, vnew[b].v)
                k.act(o1s[b].v, pc[:, 256:384], AF.Copy, scale=egp[:, h:h + 1])
                k.tt("dve", o_tok[:, h, :], pc[:, 384:512], o1s[b].v, ALU.add)
                k.mm(pc[:, 0:128], kdec[:, h, :], vnew[b].v)
                k.stt(Sst[:, h, :], Sst[:, h, :], egl[:, h:h + 1], pc[:, 0:128], ALU.mult, ALU.add)
                k.copy("pool", Sbf[:, h, :], Sst[:, h, :])
            stage(8)
            osq = w.pp.v.r("p (h d) -> p h d", h=H)
            k.tt("pool", osq, o_tok.v, o_tok.v, ALU.mult)
            k.reduce(oss.v, osq, ALU.add)
            k.act(oss.v, oss.v, AF.Sqrt, scale=1.0 / 128.0, bias=EPS6_T(k))
            k.recip(oss.v, oss.v)
            k.tt("dve", o_tok.v, o_tok.v, oss.v.r("p (h o) -> p h o", o=1).bc([128, H, 128]), ALU.mult)
            k.tt("pool", o_tok.v, o_tok.v, nw.v.r("p (o d) -> p o d", o=1).bc([128, H, 128]), ALU.mult)
            k.tt("dve", w.y_tok.v.r("p (h d) -> p h d", h=H), o_tok.v, zs.v.r("p (h d) -> p h d", h=H), ALU.mult)
            stage(9)
            epilogue(k, w, cst, lw, x_in, w.y_tok.v, p_in, x_out, t0 + c * 128)
            stage(10)


def EPSQ_T(k):
    return _eps_tile(k, 128.0 * 1e-6, "epsq")


KINDS = ("dn", "rw", "ml", "dn")
COMMON = (("w_out", (D, D)), ("w_gate", (D, D)), ("w_proj", (DPLE, D)), ("ln_g", (D,)), ("ln_b", (D,)), ("pn_w", (D,)))
SPEC = {
    "dn": (("w_in", (D, 4112)), ("cw", (128, 24, 4)), ("wg", (D, 96)), ("gp", (96, 2)), ("nw", (128,))),
}
NCOL = {"dn": 4112, "rw": 4224, "ml": 4112}
EMIT = {"dn": emit_deltanet}


def barrier(k):
    for e in ("pe", "act", "dve", "pool", "sp"):
        k._wait(e, dict(k.cnt))


def build_program(kinds, S):
    nc = bass.Bass("TRN2", target_bir_lowering=False)
    with ExitStack() as st0:
        k = K(nc, st0)
        nl = len(kinds)
        x_in = k.dram("x", (S, D), F32, "ExternalInput")
        p_in = k.dram("p", (nl, S, DPLE), F32, "ExternalInput")
        y_out = k.dram("y", (S, D), F32, "ExternalOutput")
        scr = [k.dram("xscr%d" % i, (S, D), F32, "Internal") for i in range(2)] if nl > 1 else []
        dins = []
        for li, kind in enumerate(kinds):
            din = {}
            for nm, shp in COMMON + SPEC[kind]:
                din[nm] = k.dram("l%d_%s" % (li, nm), shp, F32, "ExternalInput")
            dins.append(din)
        cst = make_consts(k)
        cur = x_in
        for li, kind in enumerate(kinds):
            dst = y_out if li == nl - 1 else scr[li % 2]
            with ExitStack() as stl:
                k.st = stl
                lw = LayerW()
                lw.ncol = NCOL[kind]
                load_common_weights(k, lw, li, dins[li])
                w = alloc_common(k, need_xT=(kind != "rw"))
                try:
                    EMIT[kind](k, cst, w, lw, dins[li], cur, V(p_in, p_in.t[li]), dst, S)
                except StopEmit:
                    k.dma("sp", w.pp.v, cur[0:128, :])
                    k.dma("sp", dst[0:128, :], w.pp.v)
                barrier(k)
            k.st = st0
            _eps_tiles.clear()
            cur = dst
        k.finish([y_out])
        print("instructions:", k.ninst, dict(k.cnt))
    return nc


def _f(a):
    return np.ascontiguousarray(np.asarray(a, dtype=np.float32))


def layer_inputs(inp, li, b=None):
    kind = KINDS[li]
    j = li // 3
    m = {"w_out": None, "w_gate": _f(inp["ple_w_gate"][li]), "w_proj": _f(inp["ple_w_proj"][li]),
         "ln_g": _f(inp["ln_g"][li]), "ln_b": _f(inp["ln_b"][li]), "pn_w": _f(inp["ple_norm_w"][li])}
    if kind == "dn":
        w_in = np.asarray(inp["dn_w_in"][j], dtype=np.float32)
        m["w_in"] = _f(w_in)
        m["w_out"] = _f(inp["dn_w_out"][j])
        cwt = np.asarray(inp["dn_conv_w"][j], dtype=np.float32)
        m["cw"] = _f(cwt.T.reshape(24, 128, 4).transpose(1, 0, 2))
        wg = np.zeros((D, 96), np.float32)
        wg[:, 0:8] = w_in[:, 4096:4104]
        wg[:, 32:40] = w_in[:, 4104:4112]
        wg[:, 64:72] = w_in[:, 4104:4112]
        m["wg"] = wg
        gp = np.zeros((96, 2), np.float32)
        for base in (32, 64):
            gp[base:base + 8, 0] = np.asarray(inp["dn_dt_bias"][j])
            gp[base:base + 8, 1] = np.asarray(inp["dn_a_log"][j])
        m["gp"] = gp
        m["nw"] = _f(inp["dn_norm_w"][j])
    elif kind == "ml":
        w_in = np.asarray(inp["ml_w_in"][j], dtype=np.float32)
        m["w_in"] = _f(w_in)
        m["w_out"] = _f(inp["ml_w_out"][j])
        cwt = np.asarray(inp["ml_conv_w"][j], dtype=np.float32)
        m["cw"] = _f(cwt.T.reshape(8, 128, 4).transpose(1, 0, 2))
        m["wgi"] = _f(w_in[:, 4096:4104])
        m["wgf"] = _f(w_in[:, 4104:4112])
        m["gp"] = _f(np.stack([np.asarray(inp["ml_i_bias"][j]), np.asarray(inp["ml_f_bias"][j])], axis=1))
        m["gnw"] = _f(inp["ml_gn_w"][j])
    elif kind == "rw":
        m["w_in"] = _f(inp["rw_w_in"][j])
        m["w_out"] = _f(inp["rw_w_out"][j])
        mu = np.asarray(inp["rw_mu"][j], dtype=np.float32)
        m["mu"] = _f(mu.reshape(6, 8, 128).transpose(2, 0, 1))
        cols = [np.asarray(inp[n][j], dtype=np.float32).reshape(-1) for n in ("rw_w0", "rw_a0", "rw_k_k", "rw_k_a", "rw_r_k")]
        cols.append(np.zeros(D, np.float32))
        pc = np.stack(cols, axis=1)
        m["pc"] = _f(pc.reshape(8, 128, 6).transpose(1, 0, 2))
        m["wlu"] = _f(inp["rw_w_lora_up"][j])
        m["alu"] = _f(inp["rw_a_lora_up"][j])
        m["gnw"] = _f(inp["rw_gn_w"][j])
        m["gnb"] = _f(inp["rw_gn_b"][j])
    return m


def emit_mlstm(k, cst, w, lw, din, x_in, p_in, x_out, S):
    H = 8
    NT = S // 512
    cw = k.sb([128, 8, 4], F32, "cw")
    k.dma("sp", cw.v, din["cw"].v)
    wgi = k.sb([128, 8, 8], BF16, "wgi")
    wgf = k.sb([128, 8, 8], BF16, "wgf")
    for kc in range(8):
        k.dma("pool", wgi[:, kc, :], din["wgi"][kc * 128:(kc + 1) * 128, :])
        k.dma("pool", wgf[:, kc, :], din["wgf"][kc * 128:(kc + 1) * 128, :])
    gp = k.sb([128, 2], F32, "gp")
    k.dma("sp", gp[0:8, :], din["gp"].v)
    k.ts("pool", gp[0:8, 1:2], gp[0:8, 1:2], -1.0, ALU.mult)
    gnw = bcast_load(k, din["gnw"].v, D, "gnw")
    sel = k.sb([128, H * 128], F32, "sel")
    k.memset("pool", sel.v, 1.0)
    k.aselect(sel.v, sel.v, [[-1, H], [0, 128]], ALU.is_equal, 0.0, 0, 1)
    iu8 = k.sb([128, 128], BF16, "iu8")
    k.ts("pool", iu8.v, cst.iu_f.v, 0.125, ALU.mult)

    qT = k.sb([128, 4, 512], BF16, "qT")
    kT = k.sb([128, 4, 512], BF16, "kT")
    xext = [k.sb([128, 515], F32, "xext") for _ in range(2)]
    acc = [k.sb([128, 512], F32, "acc") for _ in range(2)]
    hist = k.sb([128, 8, 3], F32, "hist")
    ilog = k.sb([128, 512], F32, "ilog")
    Bc = k.sb([128, 512], F32, "Bc")
    uu = k.sb([128, 512], F32, "uu")
    Mm = k.sb([128, 512], F32, "Mm")
    negM = k.sb([128, 512], F32, "negM")
    carry = k.sb([128, 2], F32, "carry")
    k.memset("pool", carry[:, 0:1], 0.0)
    k.memset("pool", carry[:, 1:2], -1e30)
    gtok = k.sb([128, 24], F32, "gtok")
    negMl = k.sb([128, 8], F32, "negMl")
    negMlp = k.sb([128, 8], F32, "negMlp")
    k.memset("pool", negMlp.v, 0.0)
    kwe = k.sb([128, 8], F32, "kwe")
    csd = k.sb([128, 8], F32, "csd")
    inter = k.sb([128, 8], F32, "inter")
    emn = k.sb([128, 8], F32, "emn")
    dd = k.sb([128, 8], F32, "dd")
    vaug = k.sb([128, H, 129], BF16, "vaug")
    k.memset("pool", vaug.v, 1.0)
    og = k.sb([128, D], F32, "og")
    zs = k.sb([128, D], BF16, "zs")
    kw = k.sb([128, H, 64], BF16, "kw")
    Cst = k.sb([128, 4, 129], F32, "Cst")
    Cbf = k.sb([128, 4, 129], BF16, "Cbf")
    k.memset("pool", Cst.v, 0.0)
    k.memset("pool", Cbf.v, 0.0)
    nd = k.sb([128, H, 129], F32, "nd")
    ho = k.sb([128, H, 128], F32, "ho")
    s1 = k.sb([128, 8], F32, "s1")
    s2 = k.sb([128, 8], F32, "s2")
    NSET = 2
    Wexp = [k.sb([128, 128], BF16, "Wexp") for _ in range(NSET)]
    WTm = [k.sb([128, 128], BF16, "WTm") for _ in range(NSET)]
    o1s = [k.sb([128, 129], F32, "o1s") for _ in range(NSET)]
    ps_a = [k.ps([128, 512], F32, "psa") for _ in range(NSET)]
    ps_b = [k.ps([128, 512], F32, "psb") for _ in range(NSET)]

    for ti in range(NT):
        t0 = ti * 512
        load_x_tile(k, w, cst, x_in, t0, 512)
        for kc in range(8):
            k.mm(w.ps_mm[0][0:8, :], wgi[:, kc, :], w.xT[:, kc, :], start=(kc == 0), stop=(kc == 7))
        for kc in range(8):
            k.mm(w.ps_mm[1][0:8, :], wgf[:, kc, :], w.xT[:, kc, :], start=(kc == 0), stop=(kc == 7))
        k.act(ilog[0:8, :], w.ps_mm[0][0:8, :], AF.Identity, bias=gp[0:8, 0:1])
        k.act(Bc[0:8, :], w.ps_mm[1][0:8, :], AF.Exp, scale=-1.0, bias=gp[0:8, 1:2])
        k.act(Bc[0:8, :], Bc[0:8, :], AF.Ln, bias=1.0)
        k.ts("pool", Bc[0:8, :], Bc[0:8, :], -1.0, ALU.mult)
        k.scan(Bc[0:8, :], ones512(k)[0:8, :], Bc[0:8, :], carry[0:8, 0:1], ALU.mult, ALU.add)
        k.tt("dve", uu[0:8, :], ilog[0:8, :], Bc[0:8, :], ALU.subtract)
        k.scan(Mm[0:8, :], uu[0:8, :], uu[0:8, :], carry[0:8, 1:2], ALU.max, ALU.max)
        k.copy("pool", carry[0:8, 0:1], Bc[0:8, 511:512])
        k.copy("pool", carry[0:8, 1:2], Mm[0:8, 511:512])
        k.ts("pool", negM[0:8, :], Mm[0:8, :], -1.0, ALU.mult)
        for cc in range(8):
            b = cc % 2
            ps = w.ps_mm[cc % 2]
            for kc in range(8):
                k.mm(ps.v, lw.w_in[:, kc, cc * 128:(cc + 1) * 128], w.xT[:, kc, :], start=(kc == 0), stop=(kc == 7))
            conv_chunk(k, ps.v, xext[b].v, hist[:, cc, :], cw[:, cc, :], acc[b].v, ti == 0)
            if cc < 4:
                k.act(qT[:, cc, :], acc[b].v, AF.Silu)
            else:
                k.act(kT[:, cc - 4, :], acc[b].v, AF.Silu)
        for c in range(4):
            cs = slice(c * 128, (c + 1) * 128)
            for part in range(3):
                for half in range(2):
                    ps = w.ps_mm[half]
                    col0 = 1024 + part * 1024 + half * 512
                    for kc in range(8):
                        k.mm(ps.v, w.xT[:, kc, cs], lw.w_in[:, kc, col0:col0 + 512], start=(kc == 0), stop=(kc == 7))
                    hs = slice(half * 512, (half + 1) * 512)
                    if part == 0:
                        k.copy("act", vaug[:, half * 4:half * 4 + 4, 0:128], ps.v.r("p (h d) -> p h d", h=4))
                    elif part == 1:
                        k.act(og[:, hs], ps.v, AF.Sigmoid)
                    else:
                        k.act(zs[:, hs], ps.v, AF.Silu)
            k.tr(w.ps_tr[:, 0:8], uu[0:8, cs], cst.id_f[0:8, 0:8])
            k.tr(w.ps_tr[:, 8:16], Mm[0:8, cs], cst.id_f[0:8, 0:8])
            k.tr(w.ps_tr[:, 16:24], Bc[0:8, cs], cst.id_f[0:8, 0:8])
            k.copy("act", gtok.v, w.ps_tr[:, 0:24])
            k.stt(inter.v, gtok[:, 8:16], -1.0, negMlp.v, ALU.mult, ALU.subtract)
            k.act(inter.v, inter.v, AF.Exp)
            k.tt("dve", emn.v, gtok[:, 8:16], gtok[:, 16:24], ALU.add)
            k.act(emn.v, emn.v, AF.Exp, scale=-1.0)
            for h in range(H):
                b = h % NSET
                pa = ps_a[b]
                k.mm(pa[:, 0:128], sel[0:8, h * 128:(h + 1) * 128], negM[0:8, cs])
                k.act(Wexp[b].v, pa[:, 0:128], AF.Exp, bias=gtok[:, h:h + 1])
                k.copy("act", negMl[:, h:h + 1], pa[:, 127:128])
                pb_ = 64 * (h % 2)
                k.mm(pa[:, 128:256], kT[pb_:pb_ + 64, h // 2, cs], qT[pb_:pb_ + 64, h // 2, cs])
                k.tt("pool", Wexp[b].v, Wexp[b].v, iu8.v, ALU.mult)
                k.tt("dve", WTm[b].v, pa[:, 128:256], Wexp[b].v, ALU.mult)
                pq = ps_b[b]
                k.mm(pq[:, 0:129], qT[pb_:pb_ + 64, h // 2, cs], Cbf[pb_:pb_ + 64, h // 2, :])
                k.mm(pq[:, 256:385], WTm[b].v, vaug[:, h, :])
                k.act(o1s[b].v, pq[:, 0:129], AF.Copy, scale=inter[:, h:h + 1])
                k.tt("dve", nd[:, h, :], pq[:, 256:385], o1s[b].v, ALU.add)
            k.tt("dve", kwe.v, gtok[:, 0:8], negMl.v, ALU.add)
            k.act(kwe.v, kwe.v, AF.Exp)
            k.tt("dve", csd.v, negMl.v, negMlp.v, ALU.subtract)
            k.act(csd.v, csd.v, AF.Exp)
            k.copy("pool", negMlp.v, negMl.v)
            for m in range(4):
                k.tr(w.ps_trb[:, m * 128:(m + 1) * 128], kT[:, m, cs], cst.id_b.v)
            k.tt("dve", kw.v, w.ps_trb[:, 0:512].r("p (h d) -> p h d", h=H), kwe.v.r("p (h o) -> p h o", o=1).bc([128, H, 64]),
                 ALU.mult)
            k.ts("pool", kw.v, kw.v, 0.125, ALU.mult)
            for h in range(H):
                b = h % NSET
                pq = ps_b[b]
                pb_ = 64 * (h % 2)
                k.mm(pq[pb_:pb_ + 64, 0:129], kw[:, h, :], vaug[:, h, :])
                k.stt(Cst[pb_:pb_ + 64, h // 2, :], Cst[pb_:pb_ + 64, h // 2, :], csd[pb_:pb_ + 64, h:h + 1],
                      pq[pb_:pb_ + 64, 0:129], ALU.mult, ALU.add)
                k.copy("pool", Cbf[pb_:pb_ + 64, h // 2, :], Cst[pb_:pb_ + 64, h // 2, :])
            k.ts("pool", dd.v.r("p (h o) -> p h o", o=1), nd[:, :, 128:129], -1.0, ALU.mult)
            k.tt("dve", dd.v.r("p (h o) -> p h o", o=1), dd.v.r("p (h o) -> p h o", o=1), nd[:, :, 128:129], ALU.max)
            k.tt("dve", dd.v, dd.v, emn.v, ALU.max)
            k.recip(dd.v, dd.v)
            k.tt("dve", ho.v, nd[:, :, 0:128], dd.v.r("p (h o) -> p h o", o=1).bc([128, H, 128]), ALU.mult)
            k.tt("pool", ho.v, ho.v, og.v.r("p (h d) -> p h d", h=H), ALU.mult)
            k.reduce(s1.v, ho.v, ALU.add)
            osq = w.pp.v.r("p (h d) -> p h d", h=H)
            k.tt("pool", osq, ho.v, ho.v, ALU.mult)
            k.reduce(s2.v, osq, ALU.add)
            k.ts("dve", s1.v, s1.v, 1.0 / 128.0, ALU.mult)
            k.tt("dve", dd.v, s1.v, s1.v, ALU.mult)
            k.stt(s2.v, s2.v, 1.0 / 128.0, dd.v, ALU.mult, ALU.subtract)
            k.act(s2.v, s2.v, AF.Sqrt, bias=EPS6_T(k))
            k.recip(s2.v, s2.v)
            k.tt("dve", ho.v, ho.v, s1.v.r("p (h o) -> p h o", o=1).bc([128, H, 128]), ALU.subtract)
            k.tt("dve", ho.v, ho.v, s2.v.r("p (h o) -> p h o", o=1).bc([128, H, 128]), ALU.mult)
            k.tt("pool", ho.v, ho.v, gnw.v.r("p (h d) -> p h d", h=H), ALU.mult)
            k.tt("dve", w.y_tok.v.r("p (h d) -> p h d", h=H), ho.v, zs.v.r("p (h d) -> p h d", h=H), ALU.mult)
            epilogue(k, w, cst, lw, x_in, w.y_tok.v, p_in, x_out, t0 + c * 128)


_ones512 = {}


def ones512(k):
    if id(k) not in _ones512 or _ones512[id(k)][1] is not k.st:
        t = k.sb([128, 512], F32, "ones512")
        k.memset("pool", t.v, 1.0)
        _ones512[id(k)] = (t, k.st)
    return _ones512[id(k)][0]


SPEC["ml"] = (("w_in", (D, 4112)), ("cw", (128, 8, 4)), ("wgi", (D, 8)), ("wgf", (D, 8)), ("gp", (8, 2)), ("gnw", (D,)))
EMIT["ml"] = emit_mlstm


def emit_rwkv(k, cst, w, lw, din, x_in, p_in, x_out, S):
    H = 16
    TTk = 128
    NT = S // TTk
    NC = TTk // 128
    mu = k.sb([128, 6, 8], F32, "mu")
    k.dma("sp", mu.v, din["mu"].v)
    pc = k.sb([128, 8, 6], F32, "pc")
    k.dma("sp", pc.v, din["pc"].v)
    wlu = k.sb([128, D], BF16, "wlu")
    k.dma("pool", wlu[0:64, :], din["wlu"].v)
    alu_ = k.sb([128, D], BF16, "alu")
    k.dma("pool", alu_[0:64, :], din["alu"].v)
    gnw = bcast_load(k, din["gnw"].v, D, "gnw")
    gnb = bcast_load(k, din["gnb"].v, D, "gnb")
    bones = k.sb([128, 128], BF16, "bones")
    k.memset("pool", bones.v, 0.0)
    k.memset("pool", bones[0:64, 0:64], 1.0)
    k.memset("pool", bones[64:128, 64:128], 1.0)
    bsel = k.sb([128, 2], F32, "bsel")
    k.memset("pool", bsel.v, 0.0)
    k.memset("pool", bsel[0:64, 0:1], 1.0)
    k.memset("pool", bsel[64:128, 1:2], 1.0)
    sl_b = k.sb([128, 128], BF16, "sl_b")
    k.copy("pool", sl_b.v, cst.sl_f.v)
    su2 = k.sb([128, 2, 128], BF16, "su2")
    k.copy("pool", su2[:, 0, :], cst.su_f.v)
    k.copy("pool", su2[:, 1, :], cst.su_f.v)
    iu2 = k.sb([128, 2, 128], BF16, "iu2")
    k.ts("pool", iu2[:, 0, :], cst.iu_f.v, -1.0, ALU.mult)
    k.copy("pool", iu2[:, 1, :], cst.iu_f.v)

    xTe = k.sb([128, 8, TTk + 1], BF16, "xTe")
    xlast = k.sb([128, 8, 1], BF16, "xlast")
    dx = k.sb([128, 8, TTk], BF16, "dx")
    xin = k.sb([128, 8, TTk], BF16, "xin")
    Lam = k.sb([128, 8, TTk], F32, "Lam")
    lwb = k.sb([128, 8, TTk], BF16, "lwb")
    a_bf = k.sb([128, 8, TTk], BF16, "a_bf")
    kp_bf = k.sb([128, 8, TTk], BF16, "kp_bf")
    rt_ = k.sb([128, 8, TTk], BF16, "rt")
    kt_ = k.sb([128, 8, TTk], BF16, "kt")
    at_ = k.sb([128, 8, TTk], BF16, "at")
    bt_ = k.sb([128, 8, TTk], BF16, "bt")
    vtok = k.sb([128, NC, D], BF16, "vtok")
    zs = k.sb([128, NC, D], BF16, "zs")
    th = k.sb([128, TTk], BF16, "th")
    NTMP = 6
    tmp = [k.sb([128, TTk], F32, "tmp") for _ in range(NTMP)]
    sqb = k.sb([128, TTk], BF16, "sqb")
    bon = k.sb([128, NC, 16], F32, "bon")
    gc = k.sb([128, 8], F32, "gc")
    Sst = k.sb([128, 8, 64], F32, "Sst")
    Sbf = k.sb([128, 8, 64], BF16, "Sbf")
    k.memset("pool", Sst.v, 0.0)
    k.memset("pool", Sbf.v, 0.0)
    ytm = k.sb([128, H, 64], F32, "ytm")
    s1 = k.sb([128, 16], F32, "s1")
    s2 = k.sb([128, 16], F32, "s2")
    s3 = k.sb([128, 16], F32, "s3")
    NSET = 2
    f32i = INV_F32["rw"]
    IDT = F32 if f32i else BF16
    QA = [k.sb([128, 2, 128], BF16, "QA") for _ in range(NSET)]
    Q0 = [k.sb([128, 128], IDT, "Q0") for _ in range(NSET)]
    P0 = [k.sb([128, 128], IDT, "P0") for _ in range(NSET)]
    AR = [k.sb([128, 2, 128], BF16, "AR") for _ in range(NSET)]
    TT = [k.sb([128, 128], IDT, "TT") for _ in range(NSET)]
    Xs = [k.sb([128, 64], IDT, "Xs") for _ in range(NSET)]
    Pm = [k.sb([128, 64], BF16, "Pm") for _ in range(NSET)]
    ibufs = [[k.sb([128, 2, 128], IDT, "ib") for _ in range(2)] for _ in range(NSET)]
    aG = k.sb([128, 128], BF16, "aG")
    kG = k.sb([128, 128], BF16, "kG")
    akt = k.sb([128, 2, 128], BF16, "akt")
    psA = [k.ps([128, 512], F32, "psA") for _ in range(NSET)]
    psI = [k.ps([128, 512], F32, "psI") for _ in range(NSET)]
    eps_gn = _eps_tile(k, 64e-5, "eps_gn")

    def mkxin(path):
        k.tt("pool", xin.v, dx.v, mu[:, path, :].r("p (c o) -> p c o", o=1).bc([128, 8, TTk]), ALU.mult)
        k.tt("dve", xin.v, xin.v, xTe[:, :, 1:TTk + 1], ALU.add)

    def proj_fm(col0, m, ncols=128):
        ps = w.ps_mm[m % 2]
        for kc in range(8):
            k.mm(ps[0:ncols, 0:TTk], lw.w_in[:, kc, col0:col0 + ncols], xin[:, kc, :], start=(kc == 0), stop=(kc == 7))
        return ps

    for ti in range(NT):
        t0 = ti * TTk
        if ti == 0:
            k.memset("pool", xTe[:, :, 0:1], 0.0)
        else:
            k.copy("pool", xTe[:, :, 0:1], xlast.v)
        for s in range(NC):
            k.dma("sp", w.x_tok.v, x_in[t0 + s * 128:t0 + (s + 1) * 128, :])
            for half in range(2):
                for j in range(4):
                    kc = half * 4 + j
                    k.tr(w.ps_tr[:, j * 128:(j + 1) * 128], w.x_tok[:, kc * 128:(kc + 1) * 128], cst.id_f.v)
                k.copy("act" if half == 0 else "dve", xTe[:, half * 4:half * 4 + 4, 1 + s * 128:1 + (s + 1) * 128],
                       w.ps_tr.v.r("p (a b) -> p a b", a=4))
        k.copy("pool", xlast.v, xTe[:, :, TTk:TTk + 1])
        k.tt("dve", dx.v, xTe[:, :, 0:TTk], xTe[:, :, 1:TTk + 1], ALU.subtract)
        mkxin(1)
        ps = proj_fm(1024, 0, 64)
        k.act(th[0:64, :], ps[0:64, 0:TTk], AF.Tanh)
        for m in range(8):
            ps = w.ps_mm[m % 2]
            k.mm(ps[:, 0:TTk], wlu[0:64, m * 128:(m + 1) * 128], th[0:64, :])
            t_ = tmp[m % 2]
            k.act(t_.v, ps[:, 0:TTk], AF.Sigmoid, bias=pc[:, m, 0:1])
            k.ts("pool", t_.v, t_.v, -math.exp(-0.5), ALU.mult)
            k.copy("pool", lwb[:, m, :], t_.v)
            for c in range(NC):
                cs = slice(c * 128, (c + 1) * 128)
                k.scan(Lam[:, m, cs], cst.ones_f.v, t_[:, cs], 0.0, ALU.mult, ALU.add)
        mkxin(4)
        ps = proj_fm(3136, 0, 64)
        k.copy("act", th[0:64, :], ps[0:64, 0:TTk])
        for m in range(8):
            ps = w.ps_mm[m % 2]
            k.mm(ps[:, 0:TTk], alu_[0:64, m * 128:(m + 1) * 128], th[0:64, :])
            k.act(a_bf[:, m, :], ps[:, 0:TTk], AF.Sigmoid, bias=pc[:, m, 1:2])
        mkxin(2)
        for m in range(8):
            ps = proj_fm(1088 + m * 128, m)
            kr, kkr, t1, e1, t2, t3 = tmp
            k.copy("act", kr.v, ps[:, 0:TTk])
            k.ts("pool", kkr.v, kr.v, pc[:, m, 2:3], ALU.mult)
            k.tt("pool", sqb.v, kkr.v, kkr.v, ALU.mult)
            ps2 = w.ps_mm[(m + 1) % 2]
            k.mm(ps2[:, 0:TTk], bones.v, sqb.v)
            k.act(t1.v, ps2[:, 0:TTk], AF.Sqrt, bias=EPS6_T(k))
            k.recip(t1.v, t1.v)
            k.tt("pool", kkr.v, kkr.v, t1.v, ALU.mult)
            k.ts("dve", t1.v, a_bf[:, m, :], 1.0, ALU.subtract, pc[:, m, 3:4], ALU.mult)
            k.tt("pool", t1.v, t1.v, kr.v, ALU.mult)
            k.tt("pool", t1.v, t1.v, kr.v, ALU.add)
            k.copy("pool", kp_bf[:, m, :], t1.v)
            k.act(e1.v, Lam[:, m, :], AF.Exp, scale=-1.0)
            k.tt("dve", kt_[:, m, :], t1.v, e1.v, ALU.mult)
            k.tt("pool", t2.v, kkr.v, a_bf[:, m, :], ALU.mult)
            k.tt("dve", at_[:, m, :], t2.v, e1.v, ALU.mult)
            k.tt("dve", t3.v, Lam[:, m, :], lwb[:, m, :], ALU.subtract)
            k.act(t3.v, t3.v, AF.Exp)
            k.tt("pool", bt_[:, m, :], kkr.v, t3.v, ALU.mult)
        mkxin(0)
        for m in range(8):
            ps = proj_fm(m * 128, m)
            rf, e3, pr = tmp[0], tmp[1], tmp[2 + (m % 2)]
            k.copy("act", rf.v, ps[:, 0:TTk])
            k.act(e3.v, Lam[:, m, :], AF.Exp)
            k.tt("dve", rt_[:, m, :], rf.v, e3.v, ALU.mult)
            k.tt("pool", pr.v, rf.v, kp_bf[:, m, :], ALU.mult)
            k.ts("pool", pr.v, pr.v, pc[:, m, 4:5], ALU.mult)
            for c in range(NC):
                k.mm(w.ps_tr[:, c * 16 + 2 * m:c * 16 + 2 * m + 2], pr[:, c * 128:(c + 1) * 128], bsel.v)
        k.copy("act", bon.v, w.ps_tr[:, 0:NC * 16].r("p (c h) -> p c h", c=NC))
        for path, col0 in ((3, 2112), (5, 3200)):
            mkxin(path)
            for c in range(NC):
                cs = slice(c * 128, (c + 1) * 128)
                for half in range(2):
                    ps = w.ps_mm[half]
                    for kc in range(8):
                        k.mm(ps.v, xin[:, kc, cs], lw.w_in[:, kc, col0 + half * 512:col0 + (half + 1) * 512],
                             start=(kc == 0), stop=(kc == 7))
                    if path == 3:
                        k.copy("act", vtok[:, c, half * 512:(half + 1) * 512], ps.v)
                    else:
                        k.act(zs[:, c, half * 512:(half + 1) * 512], ps.v, AF.Silu)
        for c in range(NC):
            cs = slice(c * 128, (c + 1) * 128)
            k.act(gc.v.r("p (m o) -> p m o", o=1), Lam[:, :, c * 128 + 127:c * 128 + 128], AF.Exp)
            for m in range(8):
                for hb in range(2):
                    h = 2 * m + hb
                    b = hb
                    pb_ = 64 * hb
                    rT = rt_[pb_:pb_ + 64, m, cs]
                    kT = kt_[pb_:pb_ + 64, m, cs]
                    aT = at_[pb_:pb_ + 64, m, cs]
                    bT = bt_[pb_:pb_ + 64, m, cs]
                    pa = psA[b]
                    pq = w.ps_mm[b]
                    k.mm(pa[:, 0:128], aT, bT)
                    k.mm(pa[:, 128:256], kT, bT)
                    k.mm(pa[:, 256:384], bT, aT)
                    k.tt("dve", QA[b][:, 1, :], pa[:, 128:256], su2[:, 1, :], ALU.mult)
                    k.tt("dve", Q0[b].v, pa[:, 0:128], su2[:, 0, :], ALU.mult)
                    k.tt("dve", P0[b].v, pa[:, 256:384], sl_b.v, ALU.mult)
                    k.mm(pq[:, 0:128], aT, rT)
                    k.mm(pq[:, 128:256], kT, rT)
                    k.tt("dve", AR[b].v, pq[:, 0:256].r("p (a b) -> p a b", a=2), iu2.v, ALU.mult)
                    tri_inverse(k, cst, Q0[b].v, P0[b].v, TT[b].v, psI[b], ibufs[b], psd=pa[:, 384:512], f32=f32i)
                    vh = vtok[:, c, h * 64:(h + 1) * 64]
                    Sh = Sbf[pb_:pb_ + 64, m, :]
                    k.mm(pq[:, 256:320], bT, Sh, start=True, stop=False)
                    k.mm(pq[:, 256:320], QA[b][:, 1, :], vh, start=False, stop=True)
                    k.copy("act", Xs[b].v, pq[:, 256:320])
                    k.mm(pq[:, 320:384], TT[b].v, Xs[b].v)
                    k.copy("act", Pm[b].v, pq[:, 320:384])
                    k.mm(pq[:, 384:448], rT, Sh, start=True, stop=False)
                    k.mm(pq[:, 384:448], AR[b][:, 0, :], Pm[b].v, start=False, stop=False)
                    k.mm(pq[:, 384:448], AR[b][:, 1, :], vh, start=False, stop=True)
                    k.copy("act", ytm[:, h, :], pq[:, 384:448])
                k.ts("pool", aG.v, at_[:, m, cs], gc[:, m:m + 1], ALU.mult, -1.0, ALU.mult)
                k.ts("pool", kG.v, kt_[:, m, cs], gc[:, m:m + 1], ALU.mult)
                k.tr(w.ps_trb[:, 0:128], aG.v, cst.id_b.v)
                k.tr(w.ps_trb[:, 128:256], kG.v, cst.id_b.v)
                k.copy("act", akt.v, w.ps_trb[:, 0:256].r("p (a b) -> p a b", a=2))
                for hb in range(2):
                    h = 2 * m + hb
                    pb_ = 64 * hb
                    pq = w.ps_mm[hb]
                    vh = vtok[:, c, h * 64:(h + 1) * 64]
                    k.mm(pq[pb_:pb_ + 64, 448:512], akt[:, 0, pb_:pb_ + 64], Pm[hb].v, start=True, stop=False)
                    k.mm(pq[pb_:pb_ + 64, 448:512], akt[:, 1, pb_:pb_ + 64], vh, start=False, stop=True)
                    k.stt(Sst[pb_:pb_ + 64, m, :], Sst[pb_:pb_ + 64, m, :], gc[pb_:pb_ + 64, m:m + 1],
                          pq[pb_:pb_ + 64, 448:512], ALU.mult, ALU.add)
                    k.copy("pool", Sbf[pb_:pb_ + 64, m, :], Sst[pb_:pb_ + 64, m, :])
            k.reduce(s1.v, ytm.v, ALU.add)
            osq = w.pp.v.r("p (h d) -> p h d", h=H)
            k.tt("pool", osq, ytm.v, ytm.v, ALU.mult)
            k.reduce(s2.v, osq, ALU.add)
            k.ts("dve", s1.v, s1.v, 1.0 / 64.0, ALU.mult)
            k.tt("dve", s3.v, s1.v, s1.v, ALU.mult)
            k.stt(s2.v, s2.v, 1.0 / 64.0, s3.v, ALU.mult, ALU.subtract)
            k.act(s2.v, s2.v, AF.Sqrt, bias=eps_gn)
            k.recip(s2.v, s2.v)
            k.tt("dve", ytm.v, ytm.v, s1.v.r("p (h o) -> p h o", o=1).bc([128, H, 64]), ALU.subtract)
            k.tt("dve", ytm.v, ytm.v, s2.v.r("p (h o) -> p h o", o=1).bc([128, H, 64]), ALU.mult)
            k.tt("pool", ytm.v, ytm.v, gnw.v.r("p (h d) -> p h d", h=H), ALU.mult)
            k.tt("pool", ytm.v, ytm.v, gnb.v.r("p (h d) -> p h d", h=H), ALU.add)
            k.tt("dve", osq, vtok[:, c, :].r("p (h d) -> p h d", h=H), bon[:, c, :].r("p (h o) -> p h o", o=1).bc([128, H, 64]),
                 ALU.mult)
            k.tt("pool", ytm.v, ytm.v, osq, ALU.add)
            k.tt("dve", w.y_tok.v.r("p (h d) -> p h d", h=H), ytm.v, zs[:, c, :].r("p (h d) -> p h d", h=H), ALU.mult)
            epilogue(k, w, cst, lw, x_in, w.y_tok.v, p_in, x_out, t0 + c * 128)


SPEC["rw"] = (("w_in", (D, 4224)), ("mu", (128, 6, 8)), ("pc", (128, 8, 6)), ("wlu", (64, D)), ("alu", (64, D)),
              ("gnw", (D,)), ("gnb", (D,)))
EMIT["rw"] = emit_rwkv


_PROG = {}


def _program(kinds, S):
    key = (tuple(kinds), S)
    if key not in _PROG:
        _PROG[key] = build_program(kinds, S)
    return _PROG[key]


def run_layers(inputs, kinds_idx, xs, S):
    kinds = tuple(KINDS[li] for li in kinds_idx)
    nc = _program(kinds, S)
    shared = {}
    for n, li in enumerate(kinds_idx):
        for nm, v in layer_inputs(inputs, li).items():
            shared["l%d_%s" % (n, nm)] = v
    in_maps = []
    for b, xb in enumerate(xs):
        m = dict(shared)
        m["x"] = np.ascontiguousarray(xb, dtype=np.float32)
        m["p"] = np.ascontiguousarray(np.asarray(inputs["p"])[list(kinds_idx), b, :S], dtype=np.float32)
        in_maps.append(m)
    res = run_bass_kernel_spmd(nc, in_maps, core_ids=list(range(len(xs))))
    return [np.asarray(r["y"]) for r in res.results]


def kernel(**inputs):
    x = np.asarray(inputs["x"], dtype=np.float32)
    B, S, _ = x.shape
    outs = run_layers(inputs, list(range(DEPTH)), [x[b] for b in range(B)], S)
    return np.stack(outs, axis=0).astype(np.float32)
```

```python
import math
from contextlib import ExitStack
import numpy as np
import concourse.bass as bass
import concourse.mybir as mybir
from concourse.bass_utils import run_bass_kernel_spmd

F32 = mybir.dt.float32
BF16 = mybir.dt.bfloat16
AF = mybir.ActivationFunctionType
ALU = mybir.AluOpType
AX = mybir.AxisListType

D = 1024
DEPTH = 4
DPLE = 256
C = 128
ALPHA = (2.0 * DEPTH) ** 0.25
LN_EPS = 1e-5
INV_F32 = {"dn": True, "rw": True}
SCHED = True


class V:
    __slots__ = ("t", "ap")

    def __init__(self, t, ap):
        self.t = t
        self.ap = ap

    def __getitem__(self, idx):
        return V(self.t, self.ap[idx])

    def r(self, pat, **kw):
        return V(self.t, self.ap.rearrange(pat, **kw))

    def bc(self, shape):
        return V(self.t, self.ap.to_broadcast(shape))


class Tl:
    __slots__ = ("t", "w", "r", "name", "psum")

    def __init__(self, t, name, psum=False):
        self.t = t
        self.w = None
        self.r = {}
        self.name = name
        self.psum = psum

    def __getitem__(self, idx):
        return V(self, self.t[idx])

    @property
    def v(self):
        return V(self, self.t[:])


class K:
    def __init__(self, nc, st):
        self.nc = nc
        self.st = st
        self.eng = {"pe": nc.tensor, "act": nc.scalar, "dve": nc.vector, "pool": nc.gpsimd, "sp": nc.sync}
        self.sem = {}
        self.cnt = {}
        for e in ("pe", "act", "dve", "pool"):
            self.sem[e] = st.enter_context(nc.semaphore("c_" + e))
            self.cnt[e] = 0
        self.nq = {"sp": 16, "act": 2, "pool": 16}
        for q in ("sp", "act", "pool"):
            for i in range(self.nq[q]):
                nm = "q_%s%d" % (q, i)
                self.sem[nm] = st.enter_context(nc.semaphore(nm))
                self.cnt[nm] = 0
        self.qrr = {"sp": 0, "act": 0, "pool": 0}
        self.known = {e: {} for e in self.eng}
        self.uid = 0
        self.ninst = 0
        self.defer = False
        self.pending = []
        self.sched_ns = 0.0

    def flush(self):
        pend = self.pending
        self.pending = []
        n = len(pend)
        if n == 0:
            return
        preds = [None] * n
        succs = [[] for _ in range(n)]
        lastw = {}
        readers = {}
        for i, (e, fn, reads, writes, cost, dm) in enumerate(pend):
            ps = set()
            for t in reads:
                j = lastw.get(id(t))
                if j is not None:
                    ps.add(j)
            for t in writes:
                j = lastw.get(id(t))
                if j is not None:
                    ps.add(j)
                for j in readers.get(id(t), ()):
                    ps.add(j)
            ps.discard(i)
            preds[i] = ps
            for j in ps:
                succs[j].append(i)
            for t in writes:
                lastw[id(t)] = i
                readers[id(t)] = []
            for t in reads:
                if lastw.get(id(t)) != i:
                    readers.setdefault(id(t), []).append(i)
        lat = [(p[5][2] if p[5] is not None else p[4]) for p in pend]
        prio = [0.0] * n
        for i in range(n - 1, -1, -1):
            m = 0.0
            for s in succs[i]:
                if prio[s] > m:
                    m = prio[s]
            prio[i] = m + lat[i]
        import heapq
        npred = [len(p) for p in preds]
        ready_t = [0.0] * n
        engs = ("pe", "act", "dve", "pool", "sp")
        heaps = {e: [] for e in engs}
        free = {e: 0.0 for e in engs}
        for i in range(n):
            if npred[i] == 0:
                heapq.heappush(heaps[pend[i][0]], (0.0, -prio[i], i))
        order = []
        done = 0
        XE = 120.0
        while done < n:
            best = None
            for e in engs:
                h = heaps[e]
                if not h:
                    continue
                f = free[e]
                cand = []
                while h and h[0][0] <= f:
                    cand.append(heapq.heappop(h))
                if cand:
                    cand.sort(key=lambda c: c[1])
                    pick = cand[0]
                    for c in cand[1:]:
                        heapq.heappush(h, (c[0], c[1], c[2]))
                    start = f
                else:
                    pick = heapq.heappop(h)
                    start = pick[0]
                if best is None or start < best[0]:
                    if best is not None:
                        heapq.heappush(heaps[best[1]], best[2])
                    best = (start, e, pick)
                else:
                    heapq.heappush(h, pick)
            start, e, pick = best
            i = pick[2]
            cost = pend[i][4]
            free[e] = start + cost
            fin = start + lat[i]
            order.append(i)
            done += 1
            for s in succs[i]:
                rt = fin + (XE if pend[s][0] != e else 0.0)
                if rt > ready_t[s]:
                    ready_t[s] = rt
                npred[s] -= 1
                if npred[s] == 0:
                    heapq.heappush(heaps[pend[s][0]], (ready_t[s], -prio[s], s))
        self.sched_ns += max(free.values())
        for i in order:
            e, fn, reads, writes, cost, dm = pend[i]
            if dm is not None:
                self._emit_dma(e, dm[0], dm[1])
            else:
                self._emit_op(e, fn, reads, writes)

    def sb(self, shape, dt, name=None):
        self.uid += 1
        name = "%s_%d" % (name or "sb", self.uid)
        return Tl(self.st.enter_context(self.nc.sbuf_tensor(name, list(shape), dt)), name)

    def ps(self, shape, dt, name=None):
        self.uid += 1
        name = "%s_%d" % (name or "ps", self.uid)
        return Tl(self.st.enter_context(self.nc.psum_tensor(name, list(shape), dt)), name, psum=True)

    def dram(self, name, shape, dt, kind):
        t = self.nc.dram_tensor(name, list(shape), dt, kind=kind)
        return Tl(t.ap(), name)

    def _deps(self, reads, writes):
        deps = {}

        def add(clk, val):
            if deps.get(clk, 0) < val:
                deps[clk] = val

        for t in reads:
            if t.w is not None:
                add(*t.w)
        for t in writes:
            if t.w is not None:
                add(*t.w)
            for clk, val in t.r.items():
                add(clk, val)
        return deps

    def _wait(self, e, deps, skip_own=False):
        kn = self.known[e]
        for clk, val in deps.items():
            if skip_own and clk == e:
                continue
            if kn.get(clk, 0) < val:
                self.eng[e].wait_ge(self.sem[clk], val)
                kn[clk] = val

    def op(self, e, fn, reads, writes, cost=None):
        px = [t for t in reads if t.psum and t not in writes]
        if px:
            writes = list(writes) + px
        if self.defer:
            self.pending.append((e, fn, list(reads), list(writes), cost if cost is not None else 300.0, None))
            return
        self._emit_op(e, fn, reads, writes)

    def _emit_op(self, e, fn, reads, writes):
        deps = self._deps(reads, writes)
        self._wait(e, deps, skip_own=(e == "pe"))
        inst = fn(self.eng[e])
        self.cnt[e] += 1
        inst.then_inc(self.sem[e], 1)
        self.ninst += 1
        c = self.cnt[e]
        for t in writes:
            t.w = (e, c)
            t.r = {}
        for t in reads:
            if t.r.get(e, 0) < c:
                t.r[e] = c

    def dma(self, q, out, in_):
        if self.defer:
            nbytes = 4.0 * out.ap.free_size() * max(1, out.ap.partition_size())
            self.pending.append((q, None, [in_.t], [out.t], 60.0, (out, in_, 2000.0 + nbytes / 150.0)))
            return
        self._emit_dma(q, out, in_)

    def _emit_dma(self, q, out, in_):
        deps = self._deps([in_.t], [out.t])
        self._wait(q, deps)
        i = self.qrr[q]
        self.qrr[q] = (i + 1) % self.nq[q]
        nm = "q_%s%d" % (q, i)
        self._wait(q, {nm: self.cnt[nm]})
        inst = self.eng[q].dma_start(out=out.ap, in_=in_.ap)
        self.cnt[nm] += 16
        inst.then_inc(self.sem[nm], 16)
        self.ninst += 1
        c = self.cnt[nm]
        out.t.w = (nm, c)
        out.t.r = {}
        if in_.t.r.get(nm, 0) < c:
            in_.t.r[nm] = c

    def finish(self, tiles):
        deps = self._deps(tiles, [])
        self._wait("sp", deps)

    def mm(self, out, lhsT, rhs, start=True, stop=True):
        ps_ = 4.0 if lhsT.ap.dtype == F32 else 1.0
        cost = 25.0 + (max(64, rhs.ap.free_size()) + lhsT.ap.free_size()) * ps_ / 2.4
        self.op("pe", lambda e: e.matmul(out.ap, lhsT.ap, rhs.ap, start=start, stop=stop),
                [lhsT.t, rhs.t], [out.t], cost)

    def tr(self, out, in_, ident):
        cost = 25.0 + (128 + in_.ap.free_size()) / 2.4
        self.op("pe", lambda e: e.transpose(out.ap, in_.ap, ident.ap), [in_.t, ident.t], [out.t], cost)

    def _ecost(self, eng, ap):
        n = ap.free_size()
        if eng == "act":
            return 170.0 + 0.75 * n
        if eng == "pool":
            return 250.0 + 1.6 * n
        return 90.0 + 1.05 * n

    def act(self, out, in_, func, scale=1.0, bias=0.0, accum=None, eng="act"):
        reads = [in_.t]
        kw = {}
        if isinstance(scale, V):
            reads.append(scale.t)
            kw["scale"] = scale.ap
        else:
            kw["scale"] = scale
        if isinstance(bias, V):
            reads.append(bias.t)
            kw["bias"] = bias.ap
        else:
            kw["bias"] = bias
        writes = [out.t]
        if accum is not None:
            writes.append(accum.t)
            kw["accum_out"] = accum.ap
        self.op("act", lambda e: e.activation(out=out.ap, in_=in_.ap, func=func, **kw), reads, writes,
                self._ecost("act", out.ap) + (100.0 if accum is not None else 0.0))

    def tt(self, eng, out, in0, in1, op):
        self.op(eng, lambda e: e.tensor_tensor(out=out.ap, in0=in0.ap, in1=in1.ap, op=op),
                [in0.t, in1.t], [out.t], self._ecost(eng, out.ap))

    def ts(self, eng, out, in0, s1, op0, s2=None, op1=None):
        reads = [in0.t]
        a1 = s1
        if isinstance(s1, V):
            reads.append(s1.t)
            a1 = s1.ap
        a2 = s2
        if isinstance(s2, V):
            reads.append(s2.t)
            a2 = s2.ap
        if op1 is None:
            self.op(eng, lambda e: e.tensor_scalar(out=out.ap, in0=in0.ap, scalar1=a1, scalar2=None, op0=op0),
                    reads, [out.t], self._ecost(eng, out.ap))
        else:
            self.op(eng, lambda e: e.tensor_scalar(out=out.ap, in0=in0.ap, scalar1=a1, scalar2=a2, op0=op0, op1=op1),
                    reads, [out.t], self._ecost(eng, out.ap))

    def stt(self, out, in0, scalar, in1, op0, op1):
        reads = [in0.t, in1.t]
        a = scalar
        if isinstance(scalar, V):
            reads.append(scalar.t)
            a = scalar.ap
        self.op("dve", lambda e: e.scalar_tensor_tensor(out=out.ap, in0=in0.ap, scalar=a, in1=in1.ap, op0=op0, op1=op1),
                reads, [out.t], self._ecost("dve", out.ap))

    def copy(self, eng, out, in_):
        if eng == "act":
            self.op("act", lambda e: e.copy(out=out.ap, in_=in_.ap), [in_.t], [out.t], self._ecost("act", out.ap))
        else:
            self.op(eng, lambda e: e.tensor_copy(out=out.ap, in_=in_.ap), [in_.t], [out.t], self._ecost(eng, out.ap))

    def memset(self, eng, out, val):
        self.op(eng, lambda e: e.memset(out.ap, val), [], [out.t], self._ecost(eng, out.ap))

    def recip(self, out, in_):
        self.op("dve", lambda e: e.reciprocal(out=out.ap, in_=in_.ap), [in_.t], [out.t], 90.0 + 2.0 * out.ap.free_size())

    def reduce(self, out, in_, op, axis=AX.X):
        self.op("dve", lambda e: e.tensor_reduce(out=out.ap, in_=in_.ap, axis=axis, op=op), [in_.t], [out.t],
                self._ecost("dve", in_.ap))

    def scan(self, out, d0, d1, init, op0, op1):
        reads = [d0.t, d1.t]
        a = init
        if isinstance(init, V):
            reads.append(init.t)
            a = init.ap
        self.op("dve", lambda e: e.tensor_tensor_scan(out=out.ap, data0=d0.ap, data1=d1.ap, initial=a, op0=op0, op1=op1),
                reads, [out.t], 90.0 + 2.1 * out.ap.free_size())

    def aselect(self, out, in_, pattern, cmp, fill, base, cm):
        self.op("pool", lambda e: e.affine_select(out=out.ap, in_=in_.ap, pattern=pattern, compare_op=cmp,
                                                   fill=fill, base=base, channel_multiplier=cm),
                [in_.t], [out.t])


class Consts:
    pass


class StopEmit(Exception):
    pass


import os as _os
_STAGE = int(_os.environ.get("KDBG_STAGE", "0"))


def stage(n):
    if _STAGE and n >= _STAGE:
        raise StopEmit()


def make_consts(k):
    c = Consts()
    c.ones_f = k.sb([128, 128], F32, "ones_f")
    k.memset("pool", c.ones_f.v, 1.0)
    c.ones_b = k.sb([128, 128], BF16, "ones_b")
    k.memset("pool", c.ones_b.v, 1.0)
    c.id_f = k.sb([128, 128], F32, "id_f")
    k.aselect(c.id_f.v, c.ones_f.v, [[-1, 128]], ALU.is_equal, 0.0, 0, 1)
    c.id_b = k.sb([128, 128], BF16, "id_b")
    k.copy("pool", c.id_b.v, c.id_f.v)
    c.iu_f = k.sb([128, 128], F32, "iu_f")
    k.aselect(c.iu_f.v, c.ones_f.v, [[1, 128]], ALU.is_ge, 0.0, 0, -1)
    c.iu_b = k.sb([128, 128], BF16, "iu_b")
    k.copy("pool", c.iu_b.v, c.iu_f.v)
    c.su_f = k.sb([128, 128], F32, "su_f")
    k.aselect(c.su_f.v, c.ones_f.v, [[1, 128]], ALU.is_gt, 0.0, 0, -1)
    c.su_b = k.sb([128, 128], BF16, "su_b")
    k.copy("pool", c.su_b.v, c.su_f.v)
    c.sl_f = k.sb([128, 128], F32, "sl_f")
    k.aselect(c.sl_f.v, c.ones_f.v, [[-1, 128]], ALU.is_gt, 0.0, 0, 1)
    return c


def bcast_load(k, dram_row, n, name, q="sp"):
    t = k.sb([128, n], F32, name)
    k.dma(q, t.v, V(dram_row.t, dram_row.ap.partition_broadcast(128)))
    return t


def col_load(k, dram_vec, n, name, base=0, tile=None, q="sp"):
    if tile is None:
        tile = k.sb([128, 1], F32, name)
    k.dma(q, tile[base:base + n, 0:1], V(dram_vec.t, dram_vec.ap.rearrange("(p o) -> p o", o=1)))
    return tile


class LayerW:
    pass


def load_common_weights(k, lw, li, din):
    ncol = lw.ncol
    lw.w_in = k.sb([128, 8, ncol], BF16, "w_in")
    src = din["w_in"]
    for kc in range(8):
        k.dma("pool", lw.w_in[:, kc, :], src[kc * 128:(kc + 1) * 128, :])
    lw.w_out = k.sb([128, 8, D], BF16, "w_out")
    for kc in range(8):
        k.dma("pool", lw.w_out[:, kc, :], din["w_out"][kc * 128:(kc + 1) * 128, :])
    lw.w_gate = k.sb([128, 8, D], BF16, "w_gate")
    for kc in range(8):
        k.dma("pool", lw.w_gate[:, kc, :], din["w_gate"][kc * 128:(kc + 1) * 128, :])
    lw.w_proj = k.sb([128, 2, D], BF16, "w_proj")
    for kc in range(2):
        k.dma("pool", lw.w_proj[:, kc, :], din["w_proj"][kc * 128:(kc + 1) * 128, :])
    lw.ln_g = bcast_load(k, din["ln_g"].v, D, "ln_g")
    lw.ln_b = bcast_load(k, din["ln_b"].v, D, "ln_b")
    lw.pn_w = bcast_load(k, din["pn_w"].v, D, "pn_w")


class Work:
    pass


def alloc_common(k, need_xT=True):
    w = Work()
    w.ps_mm = [k.ps([128, 512], F32, "psmm") for _ in range(2)]
    w.ps_tr = k.ps([128, 512], F32, "pstr")
    w.ps_trb = k.ps([128, 1024], BF16, "pstrb")
    w.x_tok = k.sb([128, D], F32, "x_tok")
    if need_xT:
        w.xT = k.sb([128, 8, 512], BF16, "xT")
    w.yT = k.sb([128, 8, 128], BF16, "yT")
    w.y_tok = k.sb([128, D], BF16, "y_tok")
    w.r_tok = k.sb([128, D], F32, "r_tok")
    w.xlnT = w.yT
    w.p_tok = k.sb([128, DPLE], F32, "p_tok")
    w.pT = k.sb([128, 2, 128], BF16, "pT")
    w.pp = k.sb([128, D], F32, "pp")
    w.st6 = k.sb([128, 2, 6], F32, "st6")
    w.mv = k.sb([128, 2], F32, "mv")
    w.sm = k.sb([128, 8], F32, "sm")
    return w


def load_x_tile(k, w, cst, x_in, t0, ntok):
    ns = ntok // 128
    for s in range(ns):
        k.dma("sp", w.x_tok.v, x_in[t0 + s * 128:t0 + (s + 1) * 128, :])
        for half in range(2):
            for j in range(4):
                kc = half * 4 + j
                k.tr(w.ps_tr[:, j * 128:(j + 1) * 128], w.x_tok[:, kc * 128:(kc + 1) * 128], cst.id_f.v)
            k.copy("act" if half == 0 else "dve", w.xT[:, half * 4:half * 4 + 4, s * 128:(s + 1) * 128],
                   w.ps_tr.v.r("p (a b) -> p a b", a=4))


def epilogue(k, w, cst, lw, x_in, y_tok, p_in, x_out, t0):
    k.dma("sp", w.x_tok.v, x_in[t0:t0 + 128, :])
    k.dma("sp", w.p_tok.v, p_in[t0:t0 + 128, :])
    k_tr_bf16_8(k, w, cst, y_tok, w.yT)
    for half in range(2):
        ps = w.ps_mm[half]
        for kc in range(8):
            k.mm(ps.v, w.yT[:, kc, :], lw.w_out[:, kc, half * 512:(half + 1) * 512], start=(kc == 0), stop=(kc == 7))
        k.stt(w.r_tok[:, half * 512:(half + 1) * 512], w.x_tok[:, half * 512:(half + 1) * 512], ALPHA, ps.v,
              ALU.mult, ALU.add)
    for half in range(2):
        k.op("dve", lambda e, half=half: e.bn_stats(out=w.st6.t[:, half, :], in_=w.r_tok.t[:, half * 512:(half + 1) * 512]),
             [w.r_tok], [w.st6])
    k.op("dve", lambda e: e.bn_aggr(out=w.mv.t[:, :], in_=w.st6.t[:].rearrange("p a b -> p (a b)")), [w.st6], [w.mv])
    k.act(w.sm[:, 0:1], w.mv[:, 1:2], AF.Sqrt, bias=LN_EPS_T(k), scale=1.0)
    k.recip(w.sm[:, 1:2], w.sm[:, 0:1])
    k.ts("dve", w.r_tok.v, w.r_tok.v, w.mv[:, 0:1], ALU.subtract, w.sm[:, 1:2], ALU.mult)
    k.tt("pool", w.r_tok.v, w.r_tok.v, lw.ln_g.v, ALU.mult)
    k.tt("pool", w.r_tok.v, w.r_tok.v, lw.ln_b.v, ALU.add)
    xln = w.r_tok
    for j in range(2):
        k.tr(w.ps_tr[:, j * 128:(j + 1) * 128], w.p_tok[:, j * 128:(j + 1) * 128], cst.id_f.v)
    k.copy("act", w.pT.v, w.ps_tr[:, 0:256].r("p (a b) -> p a b", a=2))
    for half in range(2):
        ps = w.ps_mm[half]
        for kc in range(2):
            k.mm(ps.v, w.pT[:, kc, :], lw.w_proj[:, kc, half * 512:(half + 1) * 512], start=(kc == 0), stop=(kc == 1))
        k.copy("act", w.pp[:, half * 512:(half + 1) * 512], ps.v)
    k.act(w.y_tok.v, w.pp.v, AF.Square, accum=w.sm[:, 2:3])
    k.act(w.sm[:, 3:4], w.sm[:, 2:3], AF.Sqrt, scale=1.0 / D, bias=EPS6_T(k))
    k.recip(w.sm[:, 4:5], w.sm[:, 3:4])
    k.stt(w.pp.v, w.pp.v, w.sm[:, 4:5], lw.pn_w.v, ALU.mult, ALU.mult)
    k.copy("act", w.y_tok.v, xln.v)
    k_tr_bf16_8(k, w, cst, w.y_tok.v, w.xlnT)
    for half in range(2):
        ps = w.ps_mm[half]
        hs = slice(half * 512, (half + 1) * 512)
        for kc in range(8):
            k.mm(ps.v, w.xlnT[:, kc, :], lw.w_gate[:, kc, hs], start=(kc == 0), stop=(kc == 7))
        k.act(ps.v, ps.v, AF.Sigmoid)
        k.tt("dve", w.pp[:, hs], ps.v, w.pp[:, hs], ALU.mult)
    k.tt("pool", w.pp.v, w.pp.v, xln.v, ALU.add)
    k.dma("sp", x_out[t0:t0 + 128, :], w.pp.v)


def k_tr_bf16_8(k, w, cst, src, dstT):
    for kc in range(8):
        k.tr(w.ps_trb[:, kc * 128:(kc + 1) * 128], src[:, kc * 128:(kc + 1) * 128], cst.id_b.v)
    k.copy("act", dstT.v, w.ps_trb.v.r("p (a b) -> p a b", a=8))


_eps_tiles = {}


def _eps_tile(k, val, nm):
    key = (id(k), nm)
    if key not in _eps_tiles:
        t = k.sb([128, 1], F32, nm)
        k.memset("pool", t.v, val)
        _eps_tiles[key] = t
    return _eps_tiles[key].v


def LN_EPS_T(k):
    return _eps_tile(k, LN_EPS, "eps_ln")


def EPS6_T(k):
    return _eps_tile(k, 1e-6, "eps6")


def tri_inverse(k, cst, Q0, P0, TT, ps, bufs, n=128, psd=None, f32=False):
    idt = cst.id_f if f32 else cst.id_b
    k.tt("pool", TT, idt[0:n, 0:n], Q0, ALU.subtract)
    k.mm(ps[0:n, 0:n], P0, Q0)
    k.mm(ps[0:n, n:2 * n], Q0, P0)
    k.copy("act", bufs[0][0:n, :, 0:n], ps[0:n, 0:2 * n].r("p (a b) -> p a b", a=2))
    Q, P = bufs[0][0:n, 0, 0:n], bufs[0][0:n, 1, 0:n]
    nlev = int(math.log2(n)) - 1
    if psd is None:
        psd = ps[0:n, 2 * n:3 * n]
    for s in range(1, nlev + 1):
        if s == nlev:
            k.mm(psd, P, TT)
            k.tt("dve", TT, psd, TT, ALU.add)
        else:
            k.mm(psd, P, TT)
            k.mm(ps[0:n, 0:n], P, Q)
            k.mm(ps[0:n, n:2 * n], Q, P)
            nb = bufs[s % 2]
            k.tt("dve", TT, psd, TT, ALU.add)
            k.copy("act", nb[0:n, :, 0:n], ps[0:n, 0:2 * n].r("p (a b) -> p a b", a=2))
            Q, P = nb[0:n, 0, 0:n], nb[0:n, 1, 0:n]


def conv_chunk(k, ps, xext, hist_c, cw_c, acc, first_tile):
    k.copy("act", xext[:, 3:515], ps)
    if first_tile:
        k.memset("pool", xext[:, 0:3], 0.0)
    else:
        k.copy("pool", xext[:, 0:3], hist_c)
    k.ts("dve", acc, xext[:, 3:515], cw_c[:, 3:4], ALU.mult)
    for j in (2, 1, 0):
        k.stt(acc, xext[:, j:j + 512], cw_c[:, j:j + 1], acc, ALU.mult, ALU.add)
    k.copy("pool", hist_c, xext[:, 512:515])


def emit_deltanet(k, cst, w, lw, din, x_in, p_in, x_out, S):
    H = 8
    NT = S // 512
    cw = k.sb([128, 24, 4], F32, "cw")
    k.dma("sp", cw.v, din["cw"].v)
    wg = k.sb([128, 8, 96], BF16, "wg")
    for kc in range(8):
        k.dma("pool", wg[:, kc, :], din["wg"][kc * 128:(kc + 1) * 128, :])
    gp = k.sb([128, 2], F32, "gp")
    k.dma("sp", gp[0:96, :], din["gp"].v)
    nA = k.sb([128, 1], F32, "nA")
    k.act(nA[0:96, :], gp[0:96, 1:2], AF.Exp)
    k.ts("pool", nA[0:96, :], nA[0:96, :], -1.0, ALU.mult)
    nw = bcast_load(k, din["nw"].v, 128, "nw")

    qT = k.sb([128, H, 512], BF16, "qT")
    kT = k.sb([128, H, 512], BF16, "kT")
    vT = k.sb([128, H, 512], BF16, "vT")
    xext = [k.sb([128, 515], F32, "xext")] * 2
    acc = [k.sb([128, 512], F32, "acc") for _ in range(2)]
    sq = [k.sb([128, 512], BF16, "sq")] * 2
    hist = k.sb([128, 24, 3], F32, "hist")
    zs = k.sb([128, D], BF16, "zs")
    GATE = k.sb([128, 512], F32, "GATE")
    k.memset("pool", GATE.v, 0.0)
    gtok = k.sb([128, 96], F32, "gtok")
    egp = k.sb([128, 8], F32, "egp")
    egt = k.sb([128, 8], F32, "egt")
    kd = k.sb([128, 8], F32, "kd")
    egl = k.sb([128, 8], F32, "egl")
    vtok = k.sb([128, H, 128], BF16, "vtok")
    kdec = k.sb([128, H, 128], BF16, "kdec")
    Sst = k.sb([128, H, 128], F32, "Sst")
    Sbf = k.sb([128, H, 128], BF16, "Sbf")
    k.memset("pool", Sst.v, 0.0)
    k.memset("pool", Sbf.v, 0.0)
    o_tok = k.sb([128, H, 128], F32, "o_tok")
    oss = k.sb([128, 8], F32, "oss")
    NSET = 2
    gtri = [k.sb([128, 128], F32, "gtri") for _ in range(NSET)]
    o1s = [k.sb([128, 128], F32, "o1s") for _ in range(NSET)]
    Draw = [k.sb([128, 128], BF16, "Draw") for _ in range(NSET)]
    Dincl = [k.sb([128, 128], BF16, "Dincl") for _ in range(NSET)]
    Dstr = [k.sb([128, 128], BF16, "Dstr") for _ in range(NSET)]
    f32i = INV_F32["dn"]
    IDT = F32 if f32i else BF16
    Q0 = [k.sb([128, 128], IDT, "Q0") for _ in range(NSET)]
    P0 = [k.sb([128, 128], IDT, "P0") for _ in range(NSET)]
    MT = [k.sb([128, 128], BF16, "MT") for _ in range(NSET)]
    TT = [k.sb([128, 128], IDT, "TT") for _ in range(NSET)]
    Rt = [k.sb([128, 128], IDT, "Rt") for _ in range(NSET)]
    vnew = [k.sb([128, 128], BF16, "vnew") for _ in range(NSET)]
    ibufs = [[k.sb([128, 2, 128], IDT, "ib") for _ in range(2)] for _ in range(NSET)]
    ps_inv = [k.ps([128, 512], F32, "psinv") for _ in range(NSET)]
    ps_misc = [k.ps([128, 512], F32, "psmisc") for _ in range(NSET)]

    stage(1)
    for ti in range(NT):
        t0 = ti * 512
        load_x_tile(k, w, cst, x_in, t0, 512)
        stage(2)
        ps = w.ps_mm[0]
        for kc in range(8):
            k.mm(ps[0:96, :], wg[:, kc, :], w.xT[:, kc, :], start=(kc == 0), stop=(kc == 7))
        k.act(GATE[0:8, :], ps[0:8, :], AF.Sigmoid)
        tmpg = acc[0]
        for base in (32, 64):
            sl = slice(base, base + 8)
            k.act(tmpg[sl, :], ps[sl, :], AF.Exp, bias=gp[sl, 0:1])
            k.act(tmpg[sl, :], tmpg[sl, :], AF.Ln, bias=1.0)
            k.ts("dve", GATE[sl, :], tmpg[sl, :], nA[sl, 0:1], ALU.mult)
        for c in range(4):
            cs = slice(c * 128, (c + 1) * 128)
            k.scan(GATE[64:72, cs], cst.ones_f[64:72, :], GATE[64:72, cs], 0.0, ALU.mult, ALU.add)
        stage(3)
        for cc in range(24):
            b = cc % 2
            ps = w.ps_mm[cc % 2]
            for kc in range(8):
                k.mm(ps.v, lw.w_in[:, kc, cc * 128:(cc + 1) * 128], w.xT[:, kc, :], start=(kc == 0), stop=(kc == 7))
            conv_chunk(k, ps.v, xext[b].v, hist[:, cc, :], cw[:, cc, :], acc[b].v, ti == 0)
            if cc >= 16:
                k.act(vT[:, cc - 16, :], acc[b].v, AF.Silu)
                continue
            k.act(acc[b].v, acc[b].v, AF.Silu)
            k.tt("pool", sq[b].v, acc[b].v, acc[b].v, ALU.mult)
            ps2 = w.ps_mm[(cc + 1) % 2]
            k.mm(ps2.v, cst.ones_b.v, sq[b].v)
            rs = xext[b][:, 0:512]
            if cc < 8:
                k.act(rs, ps2.v, AF.Sqrt, scale=128.0, bias=EPSQ_T(k))
                k.recip(rs, rs)
                k.tt("pool", qT[:, cc, :], acc[b].v, rs, ALU.mult)
            else:
                k.act(rs, ps2.v, AF.Sqrt, scale=1.0, bias=EPS6_T(k))
                k.recip(rs, rs)
                k.tt("pool", kT[:, cc - 8, :], acc[b].v, rs, ALU.mult)
        stage(4)
        for c in range(4):
            cs = slice(c * 128, (c + 1) * 128)
            for half in range(2):
                ps = w.ps_mm[half]
                for kc in range(8):
                    k.mm(ps.v, w.xT[:, kc, cs], lw.w_in[:, kc, 3072 + half * 512:3072 + (half + 1) * 512],
                         start=(kc == 0), stop=(kc == 7))
                k.act(zs[:, half * 512:(half + 1) * 512], ps.v, AF.Silu)
            k.tr(w.ps_tr[:, 0:96], GATE[0:96, cs], cst.id_f[0:96, 0:96])
            k.copy("act", gtok.v, w.ps_tr[:, 0:96])
            k.mm(w.ps_tr[:, 128:136], cst.ones_f.v, gtok[:, 32:40])
            k.act(egp.v, gtok[:, 64:72], AF.Exp)
            k.ts("pool", egt.v, egp.v, -1.0, ALU.mult)
            k.tt("dve", kd.v, w.ps_tr[:, 128:136], gtok[:, 64:72], ALU.subtract)
            k.act(kd.v, kd.v, AF.Exp)
            k.act(egl.v, w.ps_tr[:, 128:136], AF.Exp)
            for h in range(H):
                k.tr(w.ps_trb[:, h * 128:(h + 1) * 128], vT[:, h, cs], cst.id_b.v)
            k.copy("act", vtok.v, w.ps_trb.v.r("p (a b) -> p a b", a=H))
            for h in range(H):
                k.tr(w.ps_trb[:, h * 128:(h + 1) * 128], kT[:, h, cs], cst.id_b.v)
            k.tt("dve", kdec.v, w.ps_trb.v.r("p (a b) -> p a b", a=H), kd.v.r("p (h o) -> p h o", o=1).bc([128, H, 128]),
                 ALU.mult)
            stage(5)
            for h in range(H):
                b = h % NSET
                pm = ps_misc[b]
                k.ts("pool", gtri[b].v, cst.iu_f.v, gtok[:, 32 + h:33 + h], ALU.mult)
                k.mm(pm[:, 0:128], cst.sl_f.v, gtri[b].v)
                k.act(Draw[b].v, pm[:, 0:128], AF.Exp)
                k.tt("pool", Dincl[b].v, Draw[b].v, cst.iu_b.v, ALU.mult)
                k.tt("pool", Dstr[b].v, Draw[b].v, cst.su_b.v, ALU.mult)
                k.mm(pm[:, 128:256], kT[:, h, cs], kT[:, h, cs])
                k.stt(Q0[b].v, pm[:, 128:256], gtok[:, h:h + 1], Dstr[b].v, ALU.mult, ALU.mult)
                k.mm(pm[:, 256:384], kT[:, h, cs], qT[:, h, cs])
                k.tt("dve", MT[b].v, pm[:, 256:384], Dincl[b].v, ALU.mult)
                pi = ps_inv[b]
                if f32i:
                    k.tr(w.ps_tr[:, 0:128], Q0[b].v, cst.id_f.v)
                    k.copy("act", P0[b].v, w.ps_tr[:, 0:128])
                else:
                    k.tr(w.ps_trb[:, 0:128], Q0[b].v, cst.id_b.v)
                    k.copy("act", P0[b].v, w.ps_trb[:, 0:128])
                stage(6)
                tri_inverse(k, cst, Q0[b].v, P0[b].v, TT[b].v, pi, ibufs[b], psd=pm[:, 384:512], f32=f32i)
                stage(7)
                pc = w.ps_mm[b]
                k.mm(pc[:, 0:128], kT[:, h, cs], Sbf[:, h, :])
                k.stt(Rt[b].v, pc[:, 0:128], egt[:, h:h + 1], vtok[:, h, :], ALU.mult, ALU.add)
                k.mm(pc[:, 128:256], TT[b].v, Rt[b].v)
                k.act(vnew[b].v, pc[:, 128:256], AF.Copy, scale=gtok[:, h:h + 1])
                k.mm(pc[:, 256:384], qT[:, h, cs], Sbf[:, h, :])
                k.mm(pc[:, 384:512], MT[b].v, vnew[b].v)
                k.act(o1s[b].v, pc[:, 256:384], AF.Copy, scale=egp[:, h:h + 1])
                k.tt("dve", o_tok[:, h, :], pc[:, 384:512], o1s[b].v, ALU.add)
                k.mm(pc[:, 0:128], kdec[:, h, :], vnew[b].v)
                k.stt(Sst[:, h, :], Sst[:, h, :], egl[:, h:h + 1], pc[:, 0:128], ALU.mult, ALU.add)
                k.copy("pool", Sbf[:, h, :], Sst[:, h, :])
            stage(8)
            osq = w.pp.v.r("p (h d) -> p h d", h=H)
            k.tt("pool", osq, o_tok.v, o_tok.v, ALU.mult)
            k.reduce(oss.v, osq, ALU.add)
            k.act(oss.v, oss.v, AF.Sqrt, scale=1.0 / 128.0, bias=EPS6_T(k))
            k.recip(oss.v, oss.v)
            k.tt("dve", o_tok.v, o_tok.v, oss.v.r("p (h o) -> p h o", o=1).bc([128, H, 128]), ALU.mult)
            k.tt("pool", o_tok.v, o_tok.v, nw.v.r("p (o d) -> p o d", o=1).bc([128, H, 128]), ALU.mult)
            k.tt("dve", w.y_tok.v.r("p (h d) -> p h d", h=H), o_tok.v, zs.v.r("p (h d) -> p h d", h=H), ALU.mult)
            stage(9)
            epilogue(k, w, cst, lw, x_in, w.y_tok.v, p_in, x_out, t0 + c * 128)
            stage(10)


def EPSQ_T(k):
    return _eps_tile(k, 128.0 * 1e-6, "epsq")


KINDS = ("dn", "rw", "ml", "dn")
COMMON = (("w_out", (D, D)), ("w_gate", (D, D)), ("w_proj", (DPLE, D)), ("ln_g", (D,)), ("ln_b", (D,)), ("pn_w", (D,)))
SPEC = {
    "dn": (("w_in", (D, 4112)), ("cw", (128, 24, 4)), ("wg", (D, 96)), ("gp", (96, 2)), ("nw", (128,))),
}
NCOL = {"dn": 4112, "rw": 4224, "ml": 4112}
EMIT = {"dn": emit_deltanet}


def barrier(k):
    for e in ("pe", "act", "dve", "pool", "sp"):
        k._wait(e, dict(k.cnt))


def build_program(kinds, S):
    nc = bass.Bass("TRN2", target_bir_lowering=False)
    with ExitStack() as st0:
        k = K(nc, st0)
        nl = len(kinds)
        x_in = k.dram("x", (S, D), F32, "ExternalInput")
        p_in = k.dram("p", (nl, S, DPLE), F32, "ExternalInput")
        y_out = k.dram("y", (S, D), F32, "ExternalOutput")
        scr = [k.dram("xscr%d" % i, (S, D), F32, "Internal") for i in range(2)] if nl > 1 else []
        dins = []
        for li, kind in enumerate(kinds):
            din = {}
            for nm, shp in COMMON + SPEC[kind]:
                din[nm] = k.dram("l%d_%s" % (li, nm), shp, F32, "ExternalInput")
            dins.append(din)
        cst = make_consts(k)
        k.defer = SCHED
        cur = x_in
        for li, kind in enumerate(kinds):
            dst = y_out if li == nl - 1 else scr[li % 2]
            with ExitStack() as stl:
                k.st = stl
                lw = LayerW()
                lw.ncol = NCOL[kind]
                load_common_weights(k, lw, li, dins[li])
                w = alloc_common(k, need_xT=(kind != "rw"))
                try:
                    EMIT[kind](k, cst, w, lw, dins[li], cur, V(p_in, p_in.t[li]), dst, S)
                except StopEmit:
                    k.dma("sp", w.pp.v, cur[0:128, :])
                    k.dma("sp", dst[0:128, :], w.pp.v)
                k.flush()
                barrier(k)
            k.st = st0
            _eps_tiles.clear()
            cur = dst
        k.flush()
        k.finish([y_out])
        print("instructions:", k.ninst, "sched_est_us", k.sched_ns / 1e3)
    return nc


def _f(a):
    return np.ascontiguousarray(np.asarray(a, dtype=np.float32))


def layer_inputs(inp, li, b=None):
    kind = KINDS[li]
    j = li // 3
    m = {"w_out": None, "w_gate": _f(inp["ple_w_gate"][li]), "w_proj": _f(inp["ple_w_proj"][li]),
         "ln_g": _f(inp["ln_g"][li]), "ln_b": _f(inp["ln_b"][li]), "pn_w": _f(inp["ple_norm_w"][li])}
    if kind == "dn":
        w_in = np.asarray(inp["dn_w_in"][j], dtype=np.float32)
        m["w_in"] = _f(w_in)
        m["w_out"] = _f(inp["dn_w_out"][j])
        cwt = np.asarray(inp["dn_conv_w"][j], dtype=np.float32)
        m["cw"] = _f(cwt.T.reshape(24, 128, 4).transpose(1, 0, 2))
        wg = np.zeros((D, 96), np.float32)
        wg[:, 0:8] = w_in[:, 4096:4104]
        wg[:, 32:40] = w_in[:, 4104:4112]
        wg[:, 64:72] = w_in[:, 4104:4112]
        m["wg"] = wg
        gp = np.zeros((96, 2), np.float32)
        for base in (32, 64):
            gp[base:base + 8, 0] = np.asarray(inp["dn_dt_bias"][j])
            gp[base:base + 8, 1] = np.asarray(inp["dn_a_log"][j])
        m["gp"] = gp
        m["nw"] = _f(inp["dn_norm_w"][j])
    elif kind == "ml":
        w_in = np.asarray(inp["ml_w_in"][j], dtype=np.float32)
        m["w_in"] = _f(w_in)
        m["w_out"] = _f(inp["ml_w_out"][j])
        cwt = np.asarray(inp["ml_conv_w"][j], dtype=np.float32)
        m["cw"] = _f(cwt.T.reshape(8, 128, 4).transpose(1, 0, 2))
        m["wgi"] = _f(w_in[:, 4096:4104])
        m["wgf"] = _f(w_in[:, 4104:4112])
        m["gp"] = _f(np.stack([np.asarray(inp["ml_i_bias"][j]), np.asarray(inp["ml_f_bias"][j])], axis=1))
        m["gnw"] = _f(inp["ml_gn_w"][j])
    elif kind == "rw":
        m["w_in"] = _f(inp["rw_w_in"][j])
        m["w_out"] = _f(inp["rw_w_out"][j])
        mu = np.asarray(inp["rw_mu"][j], dtype=np.float32)
        m["mu"] = _f(mu.reshape(6, 8, 128).transpose(2, 0, 1))
        cols = [np.asarray(inp[n][j], dtype=np.float32).reshape(-1) for n in ("rw_w0", "rw_a0", "rw_k_k", "rw_k_a", "rw_r_k")]
        cols.append(np.zeros(D, np.float32))
        pc = np.stack(cols, axis=1)
        m["pc"] = _f(pc.reshape(8, 128, 6).transpose(1, 0, 2))
        m["wlu"] = _f(inp["rw_w_lora_up"][j])
        m["alu"] = _f(inp["rw_a_lora_up"][j])
        m["gnw"] = _f(inp["rw_gn_w"][j])
        m["gnb"] = _f(inp["rw_gn_b"][j])
    return m


def emit_mlstm(k, cst, w, lw, din, x_in, p_in, x_out, S):
    H = 8
    NT = S // 512
    cw = k.sb([128, 8, 4], F32, "cw")
    k.dma("sp", cw.v, din["cw"].v)
    wgi = k.sb([128, 8, 8], BF16, "wgi")
    wgf = k.sb([128, 8, 8], BF16, "wgf")
    for kc in range(8):
        k.dma("pool", wgi[:, kc, :], din["wgi"][kc * 128:(kc + 1) * 128, :])
        k.dma("pool", wgf[:, kc, :], din["wgf"][kc * 128:(kc + 1) * 128, :])
    gp = k.sb([128, 2], F32, "gp")
    k.dma("sp", gp[0:8, :], din["gp"].v)
    k.ts("pool", gp[0:8, 1:2], gp[0:8, 1:2], -1.0, ALU.mult)
    gnw = bcast_load(k, din["gnw"].v, D, "gnw")
    sel = k.sb([128, H * 128], F32, "sel")
    k.memset("pool", sel.v, 1.0)
    k.aselect(sel.v, sel.v, [[-1, H], [0, 128]], ALU.is_equal, 0.0, 0, 1)
    iu8 = k.sb([128, 128], BF16, "iu8")
    k.ts("pool", iu8.v, cst.iu_f.v, 0.125, ALU.mult)

    qT = k.sb([128, 4, 512], BF16, "qT")
    kT = k.sb([128, 4, 512], BF16, "kT")
    xext = [k.sb([128, 515], F32, "xext") for _ in range(2)]
    acc = [k.sb([128, 512], F32, "acc") for _ in range(2)]
    hist = k.sb([128, 8, 3], F32, "hist")
    ilog = k.sb([128, 512], F32, "ilog")
    Bc = k.sb([128, 512], F32, "Bc")
    uu = k.sb([128, 512], F32, "uu")
    Mm = k.sb([128, 512], F32, "Mm")
    negM = k.sb([128, 512], F32, "negM")
    carry = k.sb([128, 2], F32, "carry")
    k.memset("pool", carry[:, 0:1], 0.0)
    k.memset("pool", carry[:, 1:2], -1e30)
    gtok = k.sb([128, 24], F32, "gtok")
    negMl = k.sb([128, 8], F32, "negMl")
    negMlp = k.sb([128, 8], F32, "negMlp")
    k.memset("pool", negMlp.v, 0.0)
    kwe = k.sb([128, 8], F32, "kwe")
    csd = k.sb([128, 8], F32, "csd")
    inter = k.sb([128, 8], F32, "inter")
    emn = k.sb([128, 8], F32, "emn")
    dd = k.sb([128, 8], F32, "dd")
    vaug = k.sb([128, H, 129], BF16, "vaug")
    k.memset("pool", vaug.v, 1.0)
    og = k.sb([128, D], F32, "og")
    zs = k.sb([128, D], BF16, "zs")
    kw = k.sb([128, H, 64], BF16, "kw")
    Cst = k.sb([128, 4, 129], F32, "Cst")
    Cbf = k.sb([128, 4, 129], BF16, "Cbf")
    k.memset("pool", Cst.v, 0.0)
    k.memset("pool", Cbf.v, 0.0)
    nd = k.sb([128, H, 129], F32, "nd")
    ho = k.sb([128, H, 128], F32, "ho")
    s1 = k.sb([128, 8], F32, "s1")
    s2 = k.sb([128, 8], F32, "s2")
    NSET = 2
    Wexp = [k.sb([128, 128], BF16, "Wexp") for _ in range(NSET)]
    WTm = [k.sb([128, 128], BF16, "WTm") for _ in range(NSET)]
    o1s = [k.sb([128, 129], F32, "o1s") for _ in range(NSET)]
    ps_a = [k.ps([128, 512], F32, "psa") for _ in range(NSET)]
    ps_b = [k.ps([128, 512], F32, "psb") for _ in range(NSET)]

    for ti in range(NT):
        t0 = ti * 512
        load_x_tile(k, w, cst, x_in, t0, 512)
        for kc in range(8):
            k.mm(w.ps_mm[0][0:8, :], wgi[:, kc, :], w.xT[:, kc, :], start=(kc == 0), stop=(kc == 7))
        for kc in range(8):
            k.mm(w.ps_mm[1][0:8, :], wgf[:, kc, :], w.xT[:, kc, :], start=(kc == 0), stop=(kc == 7))
        k.act(ilog[0:8, :], w.ps_mm[0][0:8, :], AF.Identity, bias=gp[0:8, 0:1])
        k.act(Bc[0:8, :], w.ps_mm[1][0:8, :], AF.Exp, scale=-1.0, bias=gp[0:8, 1:2])
        k.act(Bc[0:8, :], Bc[0:8, :], AF.Ln, bias=1.0)
        k.ts("pool", Bc[0:8, :], Bc[0:8, :], -1.0, ALU.mult)
        k.scan(Bc[0:8, :], ones512(k)[0:8, :], Bc[0:8, :], carry[0:8, 0:1], ALU.mult, ALU.add)
        k.tt("dve", uu[0:8, :], ilog[0:8, :], Bc[0:8, :], ALU.subtract)
        k.scan(Mm[0:8, :], uu[0:8, :], uu[0:8, :], carry[0:8, 1:2], ALU.max, ALU.max)
        k.copy("pool", carry[0:8, 0:1], Bc[0:8, 511:512])
        k.copy("pool", carry[0:8, 1:2], Mm[0:8, 511:512])
        k.ts("pool", negM[0:8, :], Mm[0:8, :], -1.0, ALU.mult)
        for cc in range(8):
            b = cc % 2
            ps = w.ps_mm[cc % 2]
            for kc in range(8):
                k.mm(ps.v, lw.w_in[:, kc, cc * 128:(cc + 1) * 128], w.xT[:, kc, :], start=(kc == 0), stop=(kc == 7))
            conv_chunk(k, ps.v, xext[b].v, hist[:, cc, :], cw[:, cc, :], acc[b].v, ti == 0)
            if cc < 4:
                k.act(qT[:, cc, :], acc[b].v, AF.Silu)
            else:
                k.act(kT[:, cc - 4, :], acc[b].v, AF.Silu)
        for c in range(4):
            cs = slice(c * 128, (c + 1) * 128)
            for part in range(3):
                for half in range(2):
                    ps = w.ps_mm[half]
                    col0 = 1024 + part * 1024 + half * 512
                    for kc in range(8):
                        k.mm(ps.v, w.xT[:, kc, cs], lw.w_in[:, kc, col0:col0 + 512], start=(kc == 0), stop=(kc == 7))
                    hs = slice(half * 512, (half + 1) * 512)
                    if part == 0:
                        k.copy("act", vaug[:, half * 4:half * 4 + 4, 0:128], ps.v.r("p (h d) -> p h d", h=4))
                    elif part == 1:
                        k.act(og[:, hs], ps.v, AF.Sigmoid)
                    else:
                        k.act(zs[:, hs], ps.v, AF.Silu)
            k.tr(w.ps_tr[:, 0:8], uu[0:8, cs], cst.id_f[0:8, 0:8])
            k.tr(w.ps_tr[:, 8:16], Mm[0:8, cs], cst.id_f[0:8, 0:8])
            k.tr(w.ps_tr[:, 16:24], Bc[0:8, cs], cst.id_f[0:8, 0:8])
            k.copy("act", gtok.v, w.ps_tr[:, 0:24])
            k.stt(inter.v, gtok[:, 8:16], -1.0, negMlp.v, ALU.mult, ALU.subtract)
            k.act(inter.v, inter.v, AF.Exp)
            k.tt("dve", emn.v, gtok[:, 8:16], gtok[:, 16:24], ALU.add)
            k.act(emn.v, emn.v, AF.Exp, scale=-1.0)
            for h in range(H):
                b = h % NSET
                pa = ps_a[b]
                pt = w.ps_tr[:, b * 128:(b + 1) * 128]
                k.mm(pt, sel[0:8, h * 128:(h + 1) * 128], negM[0:8, cs])
                k.act(Wexp[b].v, pt, AF.Exp, bias=gtok[:, h:h + 1])
                k.copy("act", negMl[:, h:h + 1], pt[:, 127:128])
                pb_ = 64 * (h % 2)
                k.mm(pa[:, 128:256], kT[pb_:pb_ + 64, h // 2, cs], qT[pb_:pb_ + 64, h // 2, cs])
                k.tt("pool", Wexp[b].v, Wexp[b].v, iu8.v, ALU.mult)
                k.tt("dve", WTm[b].v, pa[:, 128:256], Wexp[b].v, ALU.mult)
                pq = ps_b[b]
                k.mm(pq[:, 0:129], qT[pb_:pb_ + 64, h // 2, cs], Cbf[pb_:pb_ + 64, h // 2, :])
                k.mm(pq[:, 256:385], WTm[b].v, vaug[:, h, :])
                k.act(o1s[b].v, pq[:, 0:129], AF.Copy, scale=inter[:, h:h + 1])
                k.tt("dve", nd[:, h, :], pq[:, 256:385], o1s[b].v, ALU.add)
            k.tt("dve", kwe.v, gtok[:, 0:8], negMl.v, ALU.add)
            k.act(kwe.v, kwe.v, AF.Exp)
            k.tt("dve", csd.v, negMl.v, negMlp.v, ALU.subtract)
            k.act(csd.v, csd.v, AF.Exp)
            k.copy("pool", negMlp.v, negMl.v)
            for m in range(4):
                k.tr(w.ps_trb[:, m * 128:(m + 1) * 128], kT[:, m, cs], cst.id_b.v)
            k.tt("dve", kw.v, w.ps_trb[:, 0:512].r("p (h d) -> p h d", h=H), kwe.v.r("p (h o) -> p h o", o=1).bc([128, H, 64]),
                 ALU.mult)
            k.ts("pool", kw.v, kw.v, 0.125, ALU.mult)
            for h in range(H):
                b = h % NSET
                pq = ps_b[b]
                pb_ = 64 * (h % 2)
                k.mm(pq[pb_:pb_ + 64, 0:129], kw[:, h, :], vaug[:, h, :])
                k.stt(Cst[pb_:pb_ + 64, h // 2, :], Cst[pb_:pb_ + 64, h // 2, :], csd[pb_:pb_ + 64, h:h + 1],
                      pq[pb_:pb_ + 64, 0:129], ALU.mult, ALU.add)
                k.copy("pool", Cbf[pb_:pb_ + 64, h // 2, :], Cst[pb_:pb_ + 64, h // 2, :])
            k.ts("pool", dd.v.r("p (h o) -> p h o", o=1), nd[:, :, 128:129], -1.0, ALU.mult)
            k.tt("dve", dd.v.r("p (h o) -> p h o", o=1), dd.v.r("p (h o) -> p h o", o=1), nd[:, :, 128:129], ALU.max)
            k.tt("dve", dd.v, dd.v, emn.v, ALU.max)
            k.recip(dd.v, dd.v)
            k.tt("dve", ho.v, nd[:, :, 0:128], dd.v.r("p (h o) -> p h o", o=1).bc([128, H, 128]), ALU.mult)
            k.tt("pool", ho.v, ho.v, og.v.r("p (h d) -> p h d", h=H), ALU.mult)
            k.reduce(s1.v, ho.v, ALU.add)
            osq = w.pp.v.r("p (h d) -> p h d", h=H)
            k.tt("pool", osq, ho.v, ho.v, ALU.mult)
            k.reduce(s2.v, osq, ALU.add)
            k.ts("dve", s1.v, s1.v, 1.0 / 128.0, ALU.mult)
            k.tt("dve", dd.v, s1.v, s1.v, ALU.mult)
            k.stt(s2.v, s2.v, 1.0 / 128.0, dd.v, ALU.mult, ALU.subtract)
            k.act(s2.v, s2.v, AF.Sqrt, bias=EPS6_T(k))
            k.recip(s2.v, s2.v)
            k.tt("dve", ho.v, ho.v, s1.v.r("p (h o) -> p h o", o=1).bc([128, H, 128]), ALU.subtract)
            k.tt("dve", ho.v, ho.v, s2.v.r("p (h o) -> p h o", o=1).bc([128, H, 128]), ALU.mult)
            k.tt("pool", ho.v, ho.v, gnw.v.r("p (h d) -> p h d", h=H), ALU.mult)
            k.tt("dve", w.y_tok.v.r("p (h d) -> p h d", h=H), ho.v, zs.v.r("p (h d) -> p h d", h=H), ALU.mult)
            epilogue(k, w, cst, lw, x_in, w.y_tok.v, p_in, x_out, t0 + c * 128)


_ones512 = {}


def ones512(k):
    if id(k) not in _ones512 or _ones512[id(k)][1] is not k.st:
        t = k.sb([128, 512], F32, "ones512")
        k.memset("pool", t.v, 1.0)
        _ones512[id(k)] = (t, k.st)
    return _ones512[id(k)][0]


SPEC["ml"] = (("w_in", (D, 4112)), ("cw", (128, 8, 4)), ("wgi", (D, 8)), ("wgf", (D, 8)), ("gp", (8, 2)), ("gnw", (D,)))
EMIT["ml"] = emit_mlstm


def emit_rwkv(k, cst, w, lw, din, x_in, p_in, x_out, S):
    H = 16
    TTk = 128
    NT = S // TTk
    NC = TTk // 128
    mu = k.sb([128, 6, 8], F32, "mu")
    k.dma("sp", mu.v, din["mu"].v)
    pc = k.sb([128, 8, 6], F32, "pc")
    k.dma("sp", pc.v, din["pc"].v)
    wlu = k.sb([128, D], BF16, "wlu")
    k.dma("pool", wlu[0:64, :], din["wlu"].v)
    alu_ = k.sb([128, D], BF16, "alu")
    k.dma("pool", alu_[0:64, :], din["alu"].v)
    gnw = bcast_load(k, din["gnw"].v, D, "gnw")
    gnb = bcast_load(k, din["gnb"].v, D, "gnb")
    bones = k.sb([128, 128], BF16, "bones")
    k.memset("pool", bones.v, 0.0)
    k.memset("pool", bones[0:64, 0:64], 1.0)
    k.memset("pool", bones[64:128, 64:128], 1.0)
    bsel = k.sb([128, 2], F32, "bsel")
    k.memset("pool", bsel.v, 0.0)
    k.memset("pool", bsel[0:64, 0:1], 1.0)
    k.memset("pool", bsel[64:128, 1:2], 1.0)
    sl_b = k.sb([128, 128], BF16, "sl_b")
    k.copy("pool", sl_b.v, cst.sl_f.v)
    su2 = k.sb([128, 2, 128], BF16, "su2")
    k.copy("pool", su2[:, 0, :], cst.su_f.v)
    k.copy("pool", su2[:, 1, :], cst.su_f.v)
    iu2 = k.sb([128, 2, 128], BF16, "iu2")
    k.ts("pool", iu2[:, 0, :], cst.iu_f.v, -1.0, ALU.mult)
    k.copy("pool", iu2[:, 1, :], cst.iu_f.v)

    xTe = k.sb([128, 8, TTk + 1], BF16, "xTe")
    xlast = k.sb([128, 8, 1], BF16, "xlast")
    dx = k.sb([128, 8, TTk], BF16, "dx")
    xin = k.sb([128, 8, TTk], BF16, "xin")
    Lam = k.sb([128, 8, TTk], F32, "Lam")
    lwb = k.sb([128, 8, TTk], BF16, "lwb")
    a_bf = k.sb([128, 8, TTk], BF16, "a_bf")
    kp_bf = k.sb([128, 8, TTk], BF16, "kp_bf")
    rt_ = k.sb([128, 8, TTk], BF16, "rt")
    kt_ = k.sb([128, 8, TTk], BF16, "kt")
    at_ = k.sb([128, 8, TTk], BF16, "at")
    bt_ = k.sb([128, 8, TTk], BF16, "bt")
    vtok = k.sb([128, NC, D], BF16, "vtok")
    zs = k.sb([128, NC, D], BF16, "zs")
    th = k.sb([128, TTk], BF16, "th")
    NTMP = 6
    tmp = [k.sb([128, TTk], F32, "tmp") for _ in range(NTMP)]
    sqb = k.sb([128, TTk], BF16, "sqb")
    bon = k.sb([128, NC, 16], F32, "bon")
    gc = k.sb([128, 8], F32, "gc")
    Sst = k.sb([128, 8, 64], F32, "Sst")
    Sbf = k.sb([128, 8, 64], BF16, "Sbf")
    k.memset("pool", Sst.v, 0.0)
    k.memset("pool", Sbf.v, 0.0)
    ytm = k.sb([128, H, 64], F32, "ytm")
    s1 = k.sb([128, 16], F32, "s1")
    s2 = k.sb([128, 16], F32, "s2")
    s3 = k.sb([128, 16], F32, "s3")
    NSET = 2
    f32i = INV_F32["rw"]
    IDT = F32 if f32i else BF16
    QA = [k.sb([128, 2, 128], BF16, "QA") for _ in range(NSET)]
    Q0 = [k.sb([128, 128], IDT, "Q0") for _ in range(NSET)]
    P0 = [k.sb([128, 128], IDT, "P0") for _ in range(NSET)]
    AR = [k.sb([128, 2, 128], BF16, "AR") for _ in range(NSET)]
    TT = [k.sb([128, 128], IDT, "TT") for _ in range(NSET)]
    Xs = [k.sb([128, 64], IDT, "Xs") for _ in range(NSET)]
    Pm = [k.sb([128, 64], BF16, "Pm") for _ in range(NSET)]
    ibufs = [[k.sb([128, 2, 128], IDT, "ib") for _ in range(2)] for _ in range(NSET)]
    aG = k.sb([128, 128], BF16, "aG")
    kG = k.sb([128, 128], BF16, "kG")
    akt = k.sb([128, 2, 128], BF16, "akt")
    psA = [k.ps([128, 512], F32, "psA") for _ in range(NSET)]
    psI = [k.ps([128, 512], F32, "psI") for _ in range(NSET)]
    eps_gn = _eps_tile(k, 64e-5, "eps_gn")

    def mkxin(path):
        k.tt("pool", xin.v, dx.v, mu[:, path, :].r("p (c o) -> p c o", o=1).bc([128, 8, TTk]), ALU.mult)
        k.tt("dve", xin.v, xin.v, xTe[:, :, 1:TTk + 1], ALU.add)

    def proj_fm(col0, m, ncols=128):
        ps = w.ps_mm[m % 2]
        for kc in range(8):
            k.mm(ps[0:ncols, 0:TTk], lw.w_in[:, kc, col0:col0 + ncols], xin[:, kc, :], start=(kc == 0), stop=(kc == 7))
        return ps

    for ti in range(NT):
        t0 = ti * TTk
        if ti == 0:
            k.memset("pool", xTe[:, :, 0:1], 0.0)
        else:
            k.copy("pool", xTe[:, :, 0:1], xlast.v)
        for s in range(NC):
            k.dma("sp", w.x_tok.v, x_in[t0 + s * 128:t0 + (s + 1) * 128, :])
            for half in range(2):
                for j in range(4):
                    kc = half * 4 + j
                    k.tr(w.ps_tr[:, j * 128:(j + 1) * 128], w.x_tok[:, kc * 128:(kc + 1) * 128], cst.id_f.v)
                k.copy("act" if half == 0 else "dve", xTe[:, half * 4:half * 4 + 4, 1 + s * 128:1 + (s + 1) * 128],
                       w.ps_tr.v.r("p (a b) -> p a b", a=4))
        k.copy("pool", xlast.v, xTe[:, :, TTk:TTk + 1])
        k.tt("dve", dx.v, xTe[:, :, 0:TTk], xTe[:, :, 1:TTk + 1], ALU.subtract)
        mkxin(1)
        ps = proj_fm(1024, 0, 64)
        k.act(th[0:64, :], ps[0:64, 0:TTk], AF.Tanh)
        for m in range(8):
            ps = w.ps_mm[m % 2]
            k.mm(ps[:, 0:TTk], wlu[0:64, m * 128:(m + 1) * 128], th[0:64, :])
            t_ = tmp[m % 2]
            k.act(t_.v, ps[:, 0:TTk], AF.Sigmoid, bias=pc[:, m, 0:1])
            k.ts("pool", t_.v, t_.v, -math.exp(-0.5), ALU.mult)
            k.copy("pool", lwb[:, m, :], t_.v)
            for c in range(NC):
                cs = slice(c * 128, (c + 1) * 128)
                k.scan(Lam[:, m, cs], cst.ones_f.v, t_[:, cs], 0.0, ALU.mult, ALU.add)
        mkxin(4)
        ps = proj_fm(3136, 0, 64)
        k.copy("act", th[0:64, :], ps[0:64, 0:TTk])
        for m in range(8):
            ps = w.ps_mm[m % 2]
            k.mm(ps[:, 0:TTk], alu_[0:64, m * 128:(m + 1) * 128], th[0:64, :])
            k.act(a_bf[:, m, :], ps[:, 0:TTk], AF.Sigmoid, bias=pc[:, m, 1:2])
        mkxin(2)
        for m in range(8):
            ps = proj_fm(1088 + m * 128, m)
            kr, kkr, t1, e1, t2, t3 = tmp
            k.copy("act", kr.v, ps[:, 0:TTk])
            k.ts("pool", kkr.v, kr.v, pc[:, m, 2:3], ALU.mult)
            k.tt("pool", sqb.v, kkr.v, kkr.v, ALU.mult)
            ps2 = w.ps_mm[(m + 1) % 2]
            k.mm(ps2[:, 0:TTk], bones.v, sqb.v)
            k.act(t1.v, ps2[:, 0:TTk], AF.Sqrt, bias=EPS6_T(k))
            k.recip(t1.v, t1.v)
            k.tt("pool", kkr.v, kkr.v, t1.v, ALU.mult)
            k.ts("dve", t1.v, a_bf[:, m, :], 1.0, ALU.subtract, pc[:, m, 3:4], ALU.mult)
            k.tt("pool", t1.v, t1.v, kr.v, ALU.mult)
            k.tt("pool", t1.v, t1.v, kr.v, ALU.add)
            k.copy("pool", kp_bf[:, m, :], t1.v)
            k.act(e1.v, Lam[:, m, :], AF.Exp, scale=-1.0)
            k.tt("dve", kt_[:, m, :], t1.v, e1.v, ALU.mult)
            k.tt("pool", t2.v, kkr.v, a_bf[:, m, :], ALU.mult)
            k.tt("dve", at_[:, m, :], t2.v, e1.v, ALU.mult)
            k.tt("dve", t3.v, Lam[:, m, :], lwb[:, m, :], ALU.subtract)
            k.act(t3.v, t3.v, AF.Exp)
            k.tt("pool", bt_[:, m, :], kkr.v, t3.v, ALU.mult)
        mkxin(0)
        for m in range(8):
            ps = proj_fm(m * 128, m)
            rf, e3, pr = tmp[0], tmp[1], tmp[2 + (m % 2)]
            k.copy("act", rf.v, ps[:, 0:TTk])
            k.act(e3.v, Lam[:, m, :], AF.Exp)
            k.tt("dve", rt_[:, m, :], rf.v, e3.v, ALU.mult)
            k.tt("pool", pr.v, rf.v, kp_bf[:, m, :], ALU.mult)
            k.ts("pool", pr.v, pr.v, pc[:, m, 4:5], ALU.mult)
            for c in range(NC):
                k.mm(w.ps_tr[:, c * 16 + 2 * m:c * 16 + 2 * m + 2], pr[:, c * 128:(c + 1) * 128], bsel.v)
        k.copy("act", bon.v, w.ps_tr[:, 0:NC * 16].r("p (c h) -> p c h", c=NC))
        for path, col0 in ((3, 2112), (5, 3200)):
            mkxin(path)
            for c in range(NC):
                cs = slice(c * 128, (c + 1) * 128)
                for half in range(2):
                    ps = w.ps_mm[half]
                    for kc in range(8):
                        k.mm(ps.v, xin[:, kc, cs], lw.w_in[:, kc, col0 + half * 512:col0 + (half + 1) * 512],
                             start=(kc == 0), stop=(kc == 7))
                    if path == 3:
                        k.copy("act", vtok[:, c, half * 512:(half + 1) * 512], ps.v)
                    else:
                        k.act(zs[:, c, half * 512:(half + 1) * 512], ps.v, AF.Silu)
        for c in range(NC):
            cs = slice(c * 128, (c + 1) * 128)
            k.act(gc.v.r("p (m o) -> p m o", o=1), Lam[:, :, c * 128 + 127:c * 128 + 128], AF.Exp)
            for m in range(8):
                for hb in range(2):
                    h = 2 * m + hb
                    b = hb
                    pb_ = 64 * hb
                    rT = rt_[pb_:pb_ + 64, m, cs]
                    kT = kt_[pb_:pb_ + 64, m, cs]
                    aT = at_[pb_:pb_ + 64, m, cs]
                    bT = bt_[pb_:pb_ + 64, m, cs]
                    pa = psA[b]
                    pq = w.ps_mm[b]
                    k.mm(pa[:, 0:128], aT, bT)
                    k.mm(pa[:, 128:256], kT, bT)
                    k.mm(pa[:, 256:384], bT, aT)
                    k.tt("dve", QA[b][:, 1, :], pa[:, 128:256], su2[:, 1, :], ALU.mult)
                    k.tt("dve", Q0[b].v, pa[:, 0:128], su2[:, 0, :], ALU.mult)
                    k.tt("dve", P0[b].v, pa[:, 256:384], sl_b.v, ALU.mult)
                    k.mm(pq[:, 0:128], aT, rT)
                    k.mm(pq[:, 128:256], kT, rT)
                    k.tt("dve", AR[b].v, pq[:, 0:256].r("p (a b) -> p a b", a=2), iu2.v, ALU.mult)
                    tri_inverse(k, cst, Q0[b].v, P0[b].v, TT[b].v, psI[b], ibufs[b], psd=pa[:, 384:512], f32=f32i)
                    vh = vtok[:, c, h * 64:(h + 1) * 64]
                    Sh = Sbf[pb_:pb_ + 64, m, :]
                    k.mm(pq[:, 256:320], bT, Sh, start=True, stop=False)
                    k.mm(pq[:, 256:320], QA[b][:, 1, :], vh, start=False, stop=True)
                    k.copy("act", Xs[b].v, pq[:, 256:320])
                    k.mm(pq[:, 320:384], TT[b].v, Xs[b].v)
                    k.copy("act", Pm[b].v, pq[:, 320:384])
                    k.mm(pq[:, 384:448], rT, Sh, start=True, stop=False)
                    k.mm(pq[:, 384:448], AR[b][:, 0, :], Pm[b].v, start=False, stop=False)
                    k.mm(pq[:, 384:448], AR[b][:, 1, :], vh, start=False, stop=True)
                    k.copy("act", ytm[:, h, :], pq[:, 384:448])
                k.ts("pool", aG.v, at_[:, m, cs], gc[:, m:m + 1], ALU.mult, -1.0, ALU.mult)
                k.ts("pool", kG.v, kt_[:, m, cs], gc[:, m:m + 1], ALU.mult)
                k.tr(w.ps_trb[:, 0:128], aG.v, cst.id_b.v)
                k.tr(w.ps_trb[:, 128:256], kG.v, cst.id_b.v)
                k.copy("act", akt.v, w.ps_trb[:, 0:256].r("p (a b) -> p a b", a=2))
                for hb in range(2):
                    h = 2 * m + hb
                    pb_ = 64 * hb
                    pq = w.ps_mm[hb]
                    vh = vtok[:, c, h * 64:(h + 1) * 64]
                    k.mm(pq[pb_:pb_ + 64, 448:512], akt[:, 0, pb_:pb_ + 64], Pm[hb].v, start=True, stop=False)
                    k.mm(pq[pb_:pb_ + 64, 448:512], akt[:, 1, pb_:pb_ + 64], vh, start=False, stop=True)
                    k.stt(Sst[pb_:pb_ + 64, m, :], Sst[pb_:pb_ + 64, m, :], gc[pb_:pb_ + 64, m:m + 1],
                          pq[pb_:pb_ + 64, 448:512], ALU.mult, ALU.add)
                    k.copy("pool", Sbf[pb_:pb_ + 64, m, :], Sst[pb_:pb_ + 64, m, :])
            k.reduce(s1.v, ytm.v, ALU.add)
            osq = w.pp.v.r("p (h d) -> p h d", h=H)
            k.tt("pool", osq, ytm.v, ytm.v, ALU.mult)
            k.reduce(s2.v, osq, ALU.add)
            k.ts("dve", s1.v, s1.v, 1.0 / 64.0, ALU.mult)
            k.tt("dve", s3.v, s1.v, s1.v, ALU.mult)
            k.stt(s2.v, s2.v, 1.0 / 64.0, s3.v, ALU.mult, ALU.subtract)
            k.act(s2.v, s2.v, AF.Sqrt, bias=eps_gn)
            k.recip(s2.v, s2.v)
            k.tt("dve", ytm.v, ytm.v, s1.v.r("p (h o) -> p h o", o=1).bc([128, H, 64]), ALU.subtract)
            k.tt("dve", ytm.v, ytm.v, s2.v.r("p (h o) -> p h o", o=1).bc([128, H, 64]), ALU.mult)
            k.tt("pool", ytm.v, ytm.v, gnw.v.r("p (h d) -> p h d", h=H), ALU.mult)
            k.tt("pool", ytm.v, ytm.v, gnb.v.r("p (h d) -> p h d", h=H), ALU.add)
            k.tt("dve", osq, vtok[:, c, :].r("p (h d) -> p h d", h=H), bon[:, c, :].r("p (h o) -> p h o", o=1).bc([128, H, 64]),
                 ALU.mult)
            k.tt("pool", ytm.v, ytm.v, osq, ALU.add)
            k.tt("dve", w.y_tok.v.r("p (h d) -> p h d", h=H), ytm.v, zs[:, c, :].r("p (h d) -> p h d", h=H), ALU.mult)
            epilogue(k, w, cst, lw, x_in, w.y_tok.v, p_in, x_out, t0 + c * 128)


SPEC["rw"] = (("w_in", (D, 4224)), ("mu", (128, 6, 8)), ("pc", (128, 8, 6)), ("wlu", (64, D)), ("alu", (64, D)),
              ("gnw", (D,)), ("gnb", (D,)))
EMIT["rw"] = emit_rwkv


_PROG = {}


def _program(kinds, S):
    key = (tuple(kinds), S)
    if key not in _PROG:
        _PROG[key] = build_program(kinds, S)
    return _PROG[key]


def run_layers(inputs, kinds_idx, xs, S):
    kinds = tuple(KINDS[li] for li in kinds_idx)
    nc = _program(kinds, S)
    shared = {}
    for n, li in enumerate(kinds_idx):
        for nm, v in layer_inputs(inputs, li).items():
            shared["l%d_%s" % (n, nm)] = v
    in_maps = []
    for b, xb in enumerate(xs):
        m = dict(shared)
        m["x"] = np.ascontiguousarray(xb, dtype=np.float32)
        m["p"] = np.ascontiguousarray(np.asarray(inputs["p"])[list(kinds_idx), b, :S], dtype=np.float32)
        in_maps.append(m)
    res = run_bass_kernel_spmd(nc, in_maps, core_ids=list(range(len(xs))))
    return [np.asarray(r["y"]) for r in res.results]


def kernel(**inputs):
    x = np.asarray(inputs["x"], dtype=np.float32)
    B, S, _ = x.shape
    outs = run_layers(inputs, list(range(DEPTH)), [x[b] for b in range(B)], S)
    return np.stack(outs, axis=0).astype(np.float32)
```
